# Optimizing a Trainium2 kernel written in Bass

```python
import math
import jax, jax.numpy as jnp
from jax import lax
import numpy as np

D_MODEL = 4096
BATCH = 2
SEQ = 8192
DEPTH = 1

CHUNK = 64
N_META = 16
HEAD_DIM = 128
Q_BLOCK = 128
MIX_WIDTH = D_MODEL
DIFF_WIDTH = MIX_WIDTH // 2
FOX_WIDTH = MIX_WIDTH - DIFF_WIDTH
DIFF_V_DIM = 2 * HEAD_DIM
N_DIFF_HEADS = DIFF_WIDTH // DIFF_V_DIM
N_FOX_HEADS = FOX_WIDTH // HEAD_DIM
D_FF = 4 * D_MODEL
ROPE_THETA = 10000.0
LN_EPS = 1e-5
SUBLN_EPS = 1e-5
DEEPNORM_ALPHA = (2.0 * DEPTH) ** 0.25
DEEPNORM_BETA = (8.0 * DEPTH) ** -0.25

DQ_OFF = 0
DK_OFF = DQ_OFF + N_DIFF_HEADS * 2 * HEAD_DIM
DV_OFF = DK_OFF + N_DIFF_HEADS * 2 * HEAD_DIM
FQ_OFF = DV_OFF + N_DIFF_HEADS * DIFF_V_DIM
FK_OFF = FQ_OFF + N_FOX_HEADS * HEAD_DIM
FV_OFF = FK_OFF + N_FOX_HEADS * HEAD_DIM
FF_OFF = FV_OFF + N_FOX_HEADS * HEAD_DIM
IN_WIDTH = FF_OFF + N_FOX_HEADS

kernel_name = "hymba_diff_fox_deepnorm_block"


def _layer_norm(h, g, b):
    hf = h.astype(jnp.float32)
    mu = jnp.mean(hf, axis=-1, keepdims=True)
    var = jnp.mean(jnp.square(hf - mu), axis=-1, keepdims=True)
    y = (hf - mu) * lax.rsqrt(var + LN_EPS) * g.astype(jnp.float32) + b.astype(jnp.float32)
    return y.astype(h.dtype)


def _rope_tables(n):
    inv = 1.0 / (ROPE_THETA ** (jnp.arange(0, HEAD_DIM, 2, dtype=jnp.float32) / HEAD_DIM))
    ang = jnp.arange(n, dtype=jnp.float32)[:, None] * inv[None, :]
    ang = jnp.concatenate([ang, ang], axis=-1)
    return jnp.cos(ang), jnp.sin(ang)


def _apply_rope(t, cos, sin):
    t1, t2 = jnp.split(t, 2, axis=-1)
    rot = jnp.concatenate([-t2, t1], axis=-1)
    out = t.astype(jnp.float32) * cos[:, None, :] + rot.astype(jnp.float32) * sin[:, None, :]
    return out.astype(t.dtype)


def _chunk_ids(n):
    pos = jnp.arange(n)
    return jnp.where(pos < N_META, 0, 1 + (pos - N_META) // CHUNK)


def _pad_seq(a, pad):
    return jnp.pad(a, [(0, 0), (0, pad)] + [(0, 0)] * (a.ndim - 2))


def _sweep(block_fn, n_pad, batch):
    out = lax.map(block_fn, jnp.arange(n_pad // Q_BLOCK))
    out = jnp.moveaxis(out, 0, 1)
    return out.reshape(batch, n_pad, out.shape[3], out.shape[4])


def _diff_attention(q, k, v, lam, lambda_init, subln_g):
    B, L = q.shape[0], q.shape[1]
    Lp = -(-L // Q_BLOCK) * Q_BLOCK
    pad = Lp - L
    qp, kp, vp = _pad_seq(q, pad), _pad_seq(k, pad), _pad_seq(v, pad)
    cid = _chunk_ids(Lp)
    key_ok = jnp.arange(Lp) < L
    scale = HEAD_DIM ** -0.5

    def block(bi):
        start = bi * Q_BLOCK
        qb = lax.dynamic_slice_in_dim(qp, start, Q_BLOCK, axis=1)
        qc = lax.dynamic_slice_in_dim(cid, start, Q_BLOCK)
        s = jnp.einsum('bqhcd,bkhcd->bhcqk', qb, kp).astype(jnp.float32) * scale
        mask = (cid[None, :] <= qc[:, None]) & key_ok[None, :]
        s = jnp.where(mask, s, -jnp.inf)
        p = jax.nn.softmax(s, axis=-1)
        a = (p[:, :, 0] - lam * p[:, :, 1]).astype(vp.dtype)
        return jnp.einsum('bhqk,bkhe->bqhe', a, vp)

    o = _sweep(block, Lp, B)[:, :L]
    of = o.astype(jnp.float32)
    of = of * lax.rsqrt(jnp.mean(jnp.square(of), axis=-1, keepdims=True) + SUBLN_EPS)
    of = of * subln_g.astype(jnp.float32) * (1.0 - lambda_init)
    return of.astype(v.dtype).reshape(B, L, -1)


def _forgetting_attention(q, k, v, log_f):
    B, L = q.shape[0], q.shape[1]
    Lp = -(-L // Q_BLOCK) * Q_BLOCK
    pad = Lp - L
    c = jnp.cumsum(log_f, axis=1)
    cT = jnp.transpose(_pad_seq(c, pad), (0, 2, 1))
    qp, kp, vp = _pad_seq(q, pad), _pad_seq(k, pad), _pad_seq(v, pad)
    idx = jnp.arange(Lp)
    scale = HEAD_DIM ** -0.5

    def block(bi):
        start = bi * Q_BLOCK
        qb = lax.dynamic_slice_in_dim(qp, start, Q_BLOCK, axis=1)
        cq = lax.dynamic_slice_in_dim(cT, start, Q_BLOCK, axis=2)
        qi = start + jnp.arange(Q_BLOCK)
        s = jnp.einsum('bqhd,bkhd->bhqk', qb, kp).astype(jnp.float32) * scale
        s = s + (cq[..., :, None] - cT[..., None, :])
        mask = idx[None, :] <= qi[:, None]
        s = jnp.where(mask, s, -jnp.inf)
        p = jax.nn.softmax(s, axis=-1).astype(vp.dtype)
        return jnp.einsum('bhqk,bkhd->bqhd', p, vp)

    o = _sweep(block, Lp, B)[:, :L]
    return o.reshape(B, L, -1)


def setup_inputs(seed: int = 0) -> dict:
    key = jax.random.key(seed)
    ks = jax.random.split(key, 20)
    f32 = jnp.float32
    x = jax.random.normal(ks[0], (BATCH, SEQ, D_MODEL), f32)
    meta_tokens = jax.random.normal(ks[1], (N_META, D_MODEL), f32)
    ln_in_g = 1.0 + 0.02 * jax.random.normal(ks[2], (D_MODEL,), f32)
    ln_in_b = 0.02 * jax.random.normal(ks[3], (D_MODEL,), f32)
    col_scale = jnp.ones((IN_WIDTH,), f32)
    col_scale = col_scale.at[DV_OFF:FQ_OFF].set(DEEPNORM_BETA).at[FV_OFF:FF_OFF].set(DEEPNORM_BETA)
    w_in = jax.random.normal(ks[4], (DEPTH, D_MODEL, IN_WIDTH), f32) * (D_MODEL ** -0.5) * col_scale
    b_forget = 1.0 + 3.0 * jax.random.uniform(ks[5], (DEPTH, N_FOX_HEADS), f32)
    lambda_q1 = 0.1 * jax.random.normal(ks[6], (DEPTH, HEAD_DIM), f32)
    lambda_k1 = 0.1 * jax.random.normal(ks[7], (DEPTH, HEAD_DIM), f32)
    lambda_q2 = 0.1 * jax.random.normal(ks[8], (DEPTH, HEAD_DIM), f32)
    lambda_k2 = 0.1 * jax.random.normal(ks[9], (DEPTH, HEAD_DIM), f32)
    subln_g = 1.0 + 0.02 * jax.random.normal(ks[10], (DEPTH, DIFF_V_DIM), f32)
    w_out = jax.random.normal(ks[11], (DEPTH, MIX_WIDTH, D_MODEL), f32) * (MIX_WIDTH ** -0.5) * DEEPNORM_BETA
    ln_attn_g = 1.0 + 0.02 * jax.random.normal(ks[12], (DEPTH, D_MODEL), f32)
    ln_attn_b = 0.02 * jax.random.normal(ks[13], (DEPTH, D_MODEL), f32)
    w_up = jax.random.normal(ks[14], (DEPTH, D_MODEL, D_FF), f32) * (D_MODEL ** -0.5) * DEEPNORM_BETA
    w_down = jax.random.normal(ks[15], (DEPTH, D_FF, D_MODEL), f32) * (D_FF ** -0.5) * DEEPNORM_BETA
    ln_mlp_g = 1.0 + 0.02 * jax.random.normal(ks[16], (DEPTH, D_MODEL), f32)
    ln_mlp_b = 0.02 * jax.random.normal(ks[17], (DEPTH, D_MODEL), f32)
    return {"x": x, "meta_tokens": meta_tokens, "ln_in_g": ln_in_g, "ln_in_b": ln_in_b,
            "w_in": w_in, "b_forget": b_forget, "lambda_q1": lambda_q1, "lambda_k1": lambda_k1,
            "lambda_q2": lambda_q2, "lambda_k2": lambda_k2, "subln_g": subln_g, "w_out": w_out,
            "ln_attn_g": ln_attn_g, "ln_attn_b": ln_attn_b, "w_up": w_up, "w_down": w_down,
            "ln_mlp_g": ln_mlp_g, "ln_mlp_b": ln_mlp_b}


def reference(x, meta_tokens, ln_in_g, ln_in_b, w_in, b_forget, lambda_q1, lambda_k1,
              lambda_q2, lambda_k2, subln_g, w_out, ln_attn_g, ln_attn_b, w_up, w_down,
              ln_mlp_g, ln_mlp_b):
    B = x.shape[0]
    meta = jnp.broadcast_to(meta_tokens[None].astype(x.dtype), (B, N_META, x.shape[2]))
    h = jnp.concatenate([meta, x], axis=1)
    h = _layer_norm(h, ln_in_g, ln_in_b)
    L = h.shape[1]
    cos, sin = _rope_tables(L)

    for li in range(DEPTH):
        lambda_init = 0.8 - 0.6 * math.exp(-0.3 * li)
        proj = jnp.einsum('bld,de->ble', h, w_in[li])
        dq = _apply_rope(proj[..., DQ_OFF:DK_OFF].reshape(B, L, 2 * N_DIFF_HEADS, HEAD_DIM), cos, sin)
        dk = _apply_rope(proj[..., DK_OFF:DV_OFF].reshape(B, L, 2 * N_DIFF_HEADS, HEAD_DIM), cos, sin)
        dq = dq.reshape(B, L, N_DIFF_HEADS, 2, HEAD_DIM)
        dk = dk.reshape(B, L, N_DIFF_HEADS, 2, HEAD_DIM)
        dv = proj[..., DV_OFF:FQ_OFF].reshape(B, L, N_DIFF_HEADS, DIFF_V_DIM)
        lam = (jnp.exp(jnp.sum(lambda_q1[li].astype(jnp.float32) * lambda_k1[li].astype(jnp.float32)))
               - jnp.exp(jnp.sum(lambda_q2[li].astype(jnp.float32) * lambda_k2[li].astype(jnp.float32)))
               + lambda_init)
        diff_out = _diff_attention(dq, dk, dv, lam, lambda_init, subln_g[li])
        fq = proj[..., FQ_OFF:FK_OFF].reshape(B, L, N_FOX_HEADS, HEAD_DIM)
        fk = proj[..., FK_OFF:FV_OFF].reshape(B, L, N_FOX_HEADS, HEAD_DIM)
        fv = proj[..., FV_OFF:FF_OFF].reshape(B, L, N_FOX_HEADS, HEAD_DIM)
        log_f = jax.nn.log_sigmoid(proj[..., FF_OFF:].astype(jnp.float32)
                                   + b_forget[li].astype(jnp.float32))
        fox_out = _forgetting_attention(fq, fk, fv, log_f)
        mix = jnp.einsum('ble,ed->bld', jnp.concatenate([diff_out, fox_out], axis=-1), w_out[li])
        h = _layer_norm(DEEPNORM_ALPHA * h + mix, ln_attn_g[li], ln_attn_b[li])
        up = jnp.einsum('bld,df->blf', h, w_up[li])
        ff = jnp.einsum('blf,fd->bld', jnp.square(jax.nn.relu(up)), w_down[li])
        h = _layer_norm(DEEPNORM_ALPHA * h + ff, ln_mlp_g[li], ln_mlp_b[li])

    return h[:, N_META:, :]
```

```python
import contextlib
import math
import numpy as np
import concourse.bass as bass
import concourse.mybir as mybir
from concourse.bass_utils import run_bass_kernel_spmd

F32 = mybir.dt.float32
BF16 = mybir.dt.bfloat16
AF = mybir.ActivationFunctionType
ALU = mybir.AluOpType
AX = mybir.AxisListType

D = 4096
SEQ = 8192
NMETA = 16
L = SEQ + NMETA
HD = 128
DFF = 16384
NCOL = 3076
ALPHA = 2.0 ** 0.25
LAMBDA_INIT = 0.8 - 0.6 * math.exp(0.0)
SCALE = HD ** -0.5
EPS = 1e-5
DQ_OFF = 0
DK_OFF = 2048
DV_OFF = 4096
FQ_OFF = 6144
FK_OFF = 8192
FV_OFF = 10240
FF_OFF = 12288


class Sched:
    ENGINES = ("pe", "act", "dve", "pool", "sp")
    NDSEM = 8

    def __init__(self):
        self.ops = []
        self.last_writer = {}
        self.readers = {}
        self.dma_count = {e: 0 for e in self.ENGINES}
        self.dma_hist = {e: [] for e in self.ENGINES}
        self.ncc = 0
        self.sp_prologue = None

    def op(self, eng, fn, reads=(), writes=(), kind="c"):
        oid = len(self.ops)
        deps = set()
        for r in reads:
            w = self.last_writer.get(r)
            if w is not None:
                deps.add(w)
        for w_ in writes:
            w = self.last_writer.get(w_)
            if w is not None:
                deps.add(w)
            for rd in self.readers.get(w_, ()):
                deps.add(rd)
        rec = dict(id=oid, eng=eng, fn=fn, kind=kind, deps=deps, signal=False, ticket=None)
        if kind == "d":
            j = self.dma_count[eng]
            self.dma_count[eng] += 1
            hist = self.dma_hist[eng]
            if j >= self.NDSEM:
                deps.add(hist[j - self.NDSEM])
            hist.append(oid)
            rec["dma_idx"] = j
            rec["signal"] = True
        if kind == "cc":
            rec["cc_idx"] = self.ncc
            self.ncc += 1
            rec["signal"] = True
        deps.discard(oid)
        self.ops.append(rec)
        for r in reads:
            self.readers.setdefault(r, []).append(oid)
        for w_ in writes:
            self.last_writer[w_] = oid
            self.readers[w_] = []
        return oid

    def emit(self, nc):
        ops = self.ops
        for o in ops:
            for d in o["deps"]:
                Dp = ops[d]
                if Dp["kind"] == "c":
                    if Dp["eng"] == "pe" and o["eng"] == "pe" and o["kind"] == "c":
                        continue
                    Dp["signal"] = True
        with contextlib.ExitStack() as es:
            esem = {e: es.enter_context(nc.semaphore("es_" + e)) for e in self.ENGINES}
            dsem = {e: [es.enter_context(nc.semaphore("ds_%s_%d" % (e, i))) for i in range(self.NDSEM)]
                    for e in self.ENGINES if self.dma_count[e] > 0}
            ccsem = [es.enter_context(nc.semaphore("cc_%d" % i)) for i in range(self.ncc)]
            cnt = {e: 0 for e in self.ENGINES}
            for o in ops:
                if o["kind"] == "c" and o["signal"]:
                    cnt[o["eng"]] += 1
                    o["ticket"] = (esem[o["eng"]], cnt[o["eng"]], "e_" + o["eng"])
                elif o["kind"] == "d":
                    j = o["dma_idx"]
                    o["ticket"] = (dsem[o["eng"]][j % self.NDSEM], 16 * (j // self.NDSEM + 1),
                                   "d_%s_%d" % (o["eng"], j % self.NDSEM))
                elif o["kind"] == "cc":
                    o["ticket"] = (ccsem[o["cc_idx"]], 1, "cc_%d" % o["cc_idx"])
            block = es.enter_context(nc.Block())
            per_eng = {e: [o for o in ops if o["eng"] == e] for e in self.ENGINES}

            def run(engname, eng):
                waited = {}
                for o in per_eng[engname]:
                    for d in sorted(o["deps"]):
                        Dp = ops[d]
                        if Dp["ticket"] is None:
                            continue
                        if Dp["kind"] == "c" and Dp["eng"] == "pe" and engname == "pe" and o["kind"] == "c":
                            continue
                        sem, val, nm = Dp["ticket"]
                        if waited.get(nm, 0) >= val:
                            continue
                        eng.wait_ge(sem, val)
                        waited[nm] = val
                    if o["kind"] == "w":
                        continue
                    ins = o["fn"](eng)
                    if o["kind"] == "d":
                        ins.then_inc(o["ticket"][0], 16)
                    elif o["kind"] == "cc":
                        ins.then_inc(o["ticket"][0])
                    elif o["signal"]:
                        ins.then_inc(o["ticket"][0], 1)

            @block.tensor
            def _(e):
                run("pe", e)

            @block.scalar
            def _(e):
                run("act", e)

            @block.vector
            def _(e):
                run("dve", e)

            @block.gpsimd
            def _(e):
                run("pool", e)

            @block.sync
            def _(e):
                with contextlib.ExitStack() as es2:
                    if self.sp_prologue is not None:
                        self.sp_prologue(e, es2)
                    run("sp", e)


class _Stop(Exception):
    pass


def build_nc(stop_after=None):
    nc = bass.Bass("TRN2", target_bir_lowering=False)
    try:
        _build_body(nc, stop_after)
    except _Stop:
        pass
    return nc


def _build_body(nc, stop_after):

    def din(name, shape, dt=F32):
        return nc.dram_tensor(name, shape, dt, kind="ExternalInput").ap()

    def dint(name, shape, dt):
        return nc.dram_tensor(name, shape, dt).ap()

    x = din("x", [2048, D])
    meta = din("meta", [NMETA, D])
    lnrow = din("lnrow", [6, D])
    w_in = din("w_in", [D, NCOL])
    big = stop_after is None or stop_after in ("p4",)
    if big:
        w_out_f = din("w_out_f", [D, D])
        w_up_f = din("w_up_f", [D, DFF])
        w_down_f = din("w_down_f", [DFF, D])
    bfor = din("bfor", [128, 4])
    lam = din("lam", [4, 128])
    sg = din("sg", [128, 2])
    consts = din("consts", [128, 7 * 128])
    cosd = din("cos", [128, L])
    sind = din("sin", [128, L])
    out = nc.dram_tensor("out", [2048, D], F32, kind="ExternalOutput").ap()

    hT_own_q = [dint("hT_own%d" % q, [128, D], BF16) for q in range(16)]
    hT_all_q = [dint("hT_all%d" % q, [4 * 128, D], BF16) for q in range(16)]
    h_own = dint("h_own", [2048, D], F32)
    w_in_b = dint("w_in_b", [D, NCOL], BF16)
    if big:
        w_out_b = dint("w_out_b", [D, D], BF16)
        w_up_b = dint("w_up_b", [D, DFF], BF16)
        w_down_b = dint("w_down_b", [DFF, D], BF16)
    QKT = dint("QKT", [16 * 128, L], BF16)
    Vd = dint("Vd", [L, 1024], BF16)
    attn_own_q = [dint("attn_own%d" % q, [1024, 512], BF16) for q in range(16)]
    attn_all_q = [dint("attn_all%d" % q, [D, 512], BF16) for q in range(16)]

    S = Sched()

    def CMP(eng, fn, r=(), w=()):
        S.op(eng, fn, reads=r, writes=w)

    def DMA(q, o, i, r=(), w=()):
        S.op(q, lambda e, o=o, i=i: e.dma_start(out=o, in_=i), reads=r, writes=w, kind="d")

    keysets = {}

    def newkey(name):
        lst = keysets.setdefault(name, [])
        k = "%s#%d" % (name, len(lst))
        lst.append(k)
        return k

    def allkeys(name):
        return list(keysets.get(name, []))

    tok0 = nc.dram_tensor("tok0", [1, 1], mybir.dt.int32, kind="ExternalInput").ap()
    regs = {}

    def sp_prologue(e, es2):
        reg = es2.enter_context(e.register("tokreg"))
        e.reg_load(reg, tok0[0:1, 0:1])
        regs["off"] = e.snap(reg)
    S.sp_prologue = sp_prologue

    NA = 53000
    with (
        nc.sbuf_tensor("arena", [128, NA], F32) as A,
        nc.psum_tensor("ps", [128, 8 * 512], F32) as ps,
    ):
        class Alloc:
            def __init__(self):
                self.off = 0

            def f32(self, n):
                ap = A[:, self.off:self.off + n]
                self.off += n
                assert self.off <= NA, self.off
                return ap

            def bf16(self, n):
                m = (n + 1) // 2
                ap = A[:, self.off:self.off + m].bitcast(BF16)
                self.off += m
                assert self.off <= NA, self.off
                return ap[:, 0:n]

        al = Alloc()

        def bank(i):
            return ps[:, i * 512:(i + 1) * 512]

        def bankb(i):
            return ps[:, i * 512:(i + 1) * 512].bitcast(BF16)

        cst = al.f32(7 * 128)
        ident_f = cst[:, 0:128]
        tri_f = cst[:, 128:256]
        rot_f = cst[:, 256:384]
        e0_f = cst[:, 384:512]
        ones_f = cst[:, 640:768]
        mask16 = cst[:, 768:769]
        epst = cst[:, 769:770]
        onet = cst[:, 770:771]
        identb = al.bf16(128)
        trib = al.bf16(128)
        maskdb = al.bf16(128)
        onesb = al.bf16(128)
        st_l = [al.f32(48) for _ in range(3)]
        mv_l = [al.f32(2) for _ in range(3)]
        tmp1_l = [al.f32(1) for _ in range(3)]
        rstd_l = [al.f32(1) for _ in range(3)]
        ln_ctr = [0]
        bfort = al.f32(4)
        sgt = al.f32(2)
        nlam = al.f32(1)
        metaT = al.bf16(32 * 16).rearrange("p (k t) -> p k t", t=16)
        Zt = al.f32(4 * 65).rearrange("p (h t) -> p h t", t=65)
        Ct = al.f32(4 * 65).rearrange("p (h t) -> p h t", t=65)
        CREF = al.f32(4 * 65).rearrange("p (h t) -> p h t", t=65)
        scratch = al.f32(1)
        PERS = al.off

        def barrier(prev, new):
            CMP("dve", lambda e: e.memset(scratch, 0.0), w=list(prev) + list(new))

        dbg_b = dbg_f = None
        if stop_after is not None:
            dbg_b = nc.dram_tensor("dbg_b", [8, 128, 512], BF16, kind="ExternalOutput").ap()
            dbg_f = nc.dram_tensor("dbg_f", [4, 128, 512], F32, kind="ExternalOutput").ap()

        def stop(phase, dumps):
            if stop_after != phase:
                return
            ks = []
            for (kind, i, src, keys) in dumps:
                dst = (dbg_b if kind == "b" else dbg_f)[i]
                k = newkey("dbg")
                DMA("sp", dst, src, r=keys, w=[k])
            S.op("sp", None, reads=allkeys("dbg"), kind="w")
            S.emit(nc)
            raise _Stop()

        DMA("sp", cst, consts, w=["cst"])
        DMA("sp", bfort, bfor, w=["bfort"])
        DMA("sp", sgt, sg, w=["sgt"])
        CMP("dve", lambda e: e.tensor_copy(out=identb, in_=ident_f), r=["cst"], w=["identb"])
        CMP("dve", lambda e: e.tensor_copy(out=trib, in_=tri_f), r=["cst"], w=["trib"])
        CMP("dve", lambda e: e.tensor_copy(out=maskdb, in_=cst[:, 512:640]), r=["cst"], w=["maskdb"])
        CMP("dve", lambda e: e.tensor_copy(out=onesb, in_=ones_f), r=["cst"], w=["onesb"])

        win_groups = [(0, 1024), (1024, 2048), (2048, NCOL)]
        for i in range(8):
            DMA("pool", w_in_b[i * 512:(i + 1) * 512, 0:1024], w_in[i * 512:(i + 1) * 512, 0:1024], w=[newkey("w_in_b0")])

        def layer_norm(yt, np_, ykey, G, Bt, gbkey, yb=None, ybkey=None, hT_dst=None, hTkey=None, defer_T=False):
            li_ = ln_ctr[0] % 3
            ln_ctr[0] += 1
            st, mv, tmp1, rstd = st_l[li_], mv_l[li_], tmp1_l[li_], rstd_l[li_]
            kst, kmv, ktm, krs = "st%d" % li_, "mv%d" % li_, "tmp1%d" % li_, "rstd%d" % li_

            def f_stats(e):
                ins = None
                for c in range(8):
                    ins = e.bn_stats(out=st[:np_, c * 6:(c + 1) * 6], in_=yt[:, c * 512:(c + 1) * 512])
                return ins
            CMP("dve", f_stats, r=[ykey], w=[kst])
            CMP("dve", lambda e: e.bn_aggr(out=mv[:np_, :], in_=st[:np_, :]), r=[kst], w=[kmv])
            CMP("act", lambda e: e.activation(out=tmp1[:np_, :], in_=mv[:np_, 1:2], func=AF.Ln, bias=epst[:np_, :], scale=1.0),
                r=[kmv, "cst"], w=[ktm])
            CMP("act", lambda e: e.activation(out=rstd[:np_, :], in_=tmp1[:np_, :], func=AF.Exp, scale=-0.5),
                r=[ktm], w=[krs])
            CMP("dve", lambda e: e.scalar_tensor_tensor(out=yt, in0=yt, scalar=mv[:np_, 0:1], in1=G[:np_, :],
                                                        op0=ALU.subtract, op1=ALU.mult), r=[ykey, kmv, gbkey], w=[ykey])
            CMP("dve", lambda e: e.scalar_tensor_tensor(out=yt, in0=yt, scalar=rstd[:np_, :], in1=Bt[:np_, :],
                                                        op0=ALU.mult, op1=ALU.add), r=[ykey, krs, gbkey], w=[ykey])
            if yb is None:
                return
            CMP("dve", lambda e: e.tensor_copy(out=yb[:np_, :], in_=yt), r=[ykey], w=[ybkey])
            if defer_T:
                return
            ln_transposes(np_, yb, ybkey, hT_dst, hTkey)

        def ln_transposes(np_, yb, ybkey, hT_dst, hTkey):
            for g in range(4):
                bk = 6 + (g % 2)
                pb = bankb(bk)

                def f_tr(e, g=g, pb=pb):
                    ins = None
                    for kk in range(8):
                        k = g * 8 + kk
                        ins = e.transpose(pb[:, kk * 128:kk * 128 + np_], yb[:np_, k * 128:(k + 1) * 128], identb[:np_, :np_])
                    return ins
                CMP("pe", f_tr, r=[ybkey, "identb"], w=["ps%d" % bk])
                src = pb.rearrange("p (k t) -> p k t", t=128)[:, :, 0:np_]
                dst = hT_dst[:, g * 8:(g + 1) * 8, 0:np_]
                CMP("act", lambda e, s=src, d=dst: e.activation(out=d, in_=s, func=AF.Copy), r=["ps%d" % bk], w=[hTkey])

        G_in = al.f32(D)
        B_in = al.f32(D)
        yts = [al.f32(D) for _ in range(3)]
        ybs = [al.bf16(D) for _ in range(2)]
        hTs = [al.bf16(32 * 128).rearrange("p (k t) -> p k t", t=128) for _ in range(2)]
        DMA("sp", G_in, lnrow[0:1, :].partition_broadcast(128), w=["gb"])
        DMA("sp", B_in, lnrow[1:2, :].partition_broadcast(128), w=["gb"])
        def allgather(src, dst, groups, rkeys, wkey):
            S.op("pool", lambda e: e.collective_compute("AllGather", ALU.bypass, replica_groups=groups,
                                                        ins=[src.opt()], outs=[dst.opt()], dma_qos="P2"),
                 reads=rkeys, writes=[wkey], kind="cc")
        G4 = [[0, 1, 2, 3], [4, 5, 6, 7]]
        G8 = [list(range(8))]
        def load_x(i):
            DMA("sp", yts[i % 3], x[i * 128:(i + 1) * 128, :], w=["yt%d" % (i % 3)])
        load_x(0)
        load_x(1)
        layer_norm(yts[0], 128, "yt0", G_in, B_in, "gb", yb=ybs[0], ybkey="yb0", defer_T=True)
        for i in range(16):
            s = i % 3
            q = i % 2
            if i + 2 < 16:
                load_x(i + 2)
            if i + 1 < 16:
                layer_norm(yts[(i + 1) % 3], 128, "yt%d" % ((i + 1) % 3), G_in, B_in, "gb", yb=ybs[(i + 1) % 2], ybkey="yb%d" % ((i + 1) % 2),
                           defer_T=True)
            DMA("sp", h_own[i * 128:(i + 1) * 128, :], yts[s], r=["yt%d" % s], w=[newkey("h_own")])
            ln_transposes(128, ybs[q], "yb%d" % q, hTs[q], "hTs%d" % q)
            DMA("sp", hT_own_q[i], hTs[q].rearrange("p k t -> p (k t)"), r=["hTs%d" % q], w=["hT_own%d" % i])
            allgather(hT_own_q[i], hT_all_q[i], G4, ["hT_own%d" % i], "hT_all%d" % i)
        DMA("sp", yts[1][0:16, :], meta, w=["yt1"])
        layer_norm(yts[1][0:16, :], 16, "yt1", G_in, B_in, "gb", yb=ybs[0], ybkey="yb0", hT_dst=metaT, hTkey="metaT")
        stop("p1b", [("b", 0, hT_all_q[0][128:256, 0:512], ["hT_all0"]),
                     ("b", 1, hT_all_q[9][384:512, 1024:1536], ["hT_all9"])])

        bg = []
        for gi in (1, 2):
            c0_, c1_ = win_groups[gi]
            for i in range(8):
                bg.append((w_in_b[i * 512:(i + 1) * 512, c0_:c1_], w_in[i * 512:(i + 1) * 512, c0_:c1_], "w_in_b%d" % gi))
        if big:
            for (srcw, dstw, nm) in ((w_out_f, w_out_b, "w_out_b"), (w_up_f, w_up_b, "w_up_b"), (w_down_f, w_down_b, "w_down_b")):
                sv = srcw.rearrange("r (a c) -> (r a) c", c=2048)
                dv = dstw.rearrange("r (a c) -> (r a) c", c=2048)
                for i in range(sv.shape[0] // 512):
                    bg.append((dv[i * 512:(i + 1) * 512, :], sv[i * 512:(i + 1) * 512, :], nm))

        def drip(n, pace=()):
            for _ in range(n):
                if not bg:
                    return
                d_, s_src, nm = bg.pop(0)
                DMA("pool", d_, s_src, r=list(pace), w=[newkey(nm)])

        al.off = PERS
        lam4 = al.f32(4 * 128).rearrange("p (v d) -> p v d", d=128)
        prod = al.f32(2 * 128).rearrange("p (v d) -> p v d", d=128)
        dots = al.f32(2)
        eds = al.f32(2)
        barrier(["yt0", "yt1", "yt2", "gb", "yb0", "yb1", "hTs0", "hTs1"], ["lam4", "prod", "dots", "eds"])
        DMA("sp", lam4, lam.partition_broadcast(128), w=["lam4"])
        CMP("dve", lambda e: e.tensor_tensor(out=prod[:, 0, :], in0=lam4[:, 0, :], in1=lam4[:, 1, :], op=ALU.mult), r=["lam4"], w=["prod"])
        CMP("dve", lambda e: e.tensor_tensor(out=prod[:, 1, :], in0=lam4[:, 2, :], in1=lam4[:, 3, :], op=ALU.mult), r=["lam4"], w=["prod"])
        CMP("dve", lambda e: e.tensor_reduce(out=dots, in_=prod, axis=AX.X, op=ALU.add), r=["prod"], w=["dots"])
        CMP("act", lambda e: e.activation(out=eds, in_=dots, func=AF.Exp), r=["dots"], w=["eds"])
        CMP("dve", lambda e: e.tensor_scalar(out=nlam, in0=eds[:, 1:2], scalar1=eds[:, 0:1], scalar2=-LAMBDA_INIT,
                                             op0=ALU.subtract, op1=ALU.add), r=["eds"], w=["nlam"])

        al.off = PERS
        wres = al.bf16(32 * 1028)
        hTb = [al.bf16(32 * 512).rearrange("p (r k t) -> p r k t", k=32, t=128) for _ in range(2)]
        stg = [al.bf16(8 * 512).rearrange("p (j t) -> p j t", t=512) for _ in range(2)]
        vst = [al.bf16(1024) for _ in range(2)]
        Tf = [al.f32(512) for _ in range(2)]
        t1 = [al.f32(512) for _ in range(2)]
        t2 = [al.f32(512) for _ in range(2)]
        cosb = [al.f32(512) for _ in range(2)]
        sinb = [al.f32(512) for _ in range(2)]
        w_in_v = w_in_b.rearrange("(k p) c -> p k c", p=128)
        QKT_v = QKT.rearrange("(j p) t -> p j t", p=128)
        prev_keys = ["yt0", "yt1", "yt2", "gb", "yb0", "yb1", "hTs0", "hTs1", "lam4", "prod", "dots", "eds"]
        barrier(prev_keys, ["wres", "hTb0", "hTb1", "stg0", "stg1", "vst0", "vst1",
                            "Tf0", "Tf1", "t10", "t11", "t20", "t21", "cs0", "cs1"])

        CMP("dve", lambda e: e.memset(Zt.rearrange("p h t -> p (h t)"), 0.0), w=["Zt"])
        blocks = [(-1, 0, 16)] + [(tb, 16 + tb * 512, 512) for tb in range(16)]
        nblk = len(blocks)
        acc_bank = [0]
        rope_pending = [None]

        def load_hT(bi, slot):
            tb, col0, n = blocks[bi]
            if tb < 0:
                return
            for rr in range(4):
                DMA("sp", hTb[slot][:, rr].rearrange("p k t -> p (k t)"), hT_all_q[tb][rr * 128:(rr + 1) * 128, :],
                    r=["hT_all%d" % tb], w=["hTb%d" % slot])

        def hT_rhs(bi, slot, k, c0, n):
            tb = blocks[bi][0]
            if tb < 0:
                return metaT[:, k, c0:c0 + n]
            if n == 512:
                return hTb[slot][:, :, k, :]
            return hTb[slot][:, c0 // 128, k, :]

        def acc_out(bk, n):
            if n == 512:
                return bank(bk).rearrange("p (r t) -> p r t", t=128)
            return bank(bk)[:, 0:n]

        for pas in range(3):
            if pas < 2:
                wv = wres[:, 0:32 * 1024].rearrange("p (k c) -> p k c", c=1024)
                for g in range(4):
                    DMA("sp", wv[:, g * 8:(g + 1) * 8, :], w_in_v[:, g * 8:(g + 1) * 8, pas * 1024:(pas + 1) * 1024],
                        r=allkeys("w_in_b%d" % pas), w=["wres"])
            else:
                wv = wres.rearrange("p (k c) -> p k c", c=1028)
                for g in range(4):
                    DMA("sp", wv[:, g * 8:(g + 1) * 8, :], w_in_v[:, g * 8:(g + 1) * 8, 2048:3076], r=allkeys("w_in_b2"), w=["wres"])
            if rope_pending[0] is not None:
                rope_pending[0]()
                rope_pending[0] = None
            load_hT(1, 1)
            for bi in range(nblk):
                tb, col0, n = blocks[bi]
                slot = bi % 2
                if bi + 1 < nblk and bi >= 1:
                    load_hT(bi + 1, (bi + 1) % 2)
                hkey = "metaT" if tb < 0 else "hTb%d" % slot
                pk = (keysets.get("QKT", []) + keysets.get("Vd", []))[-1:]
                drip(2 if pas == 0 else 1, pk)
                if pas == 0:
                    DMA("sp", cosb[slot][:, 0:n], cosd[:, col0:col0 + n], w=["cs%d" % slot])
                    DMA("sp", sinb[slot][:, 0:n], sind[:, col0:col0 + n], w=["cs%d" % slot])
                if pas < 2:
                    sg_ = stg[slot]
                    for j in range(8):
                        bk = acc_bank[0] % 4
                        acc_bank[0] += 1

                        def f_mm(e, j=j, bk=bk, bi=bi, slot=slot, n=n, wv=wv):
                            ins = None
                            for k in range(32):
                                ins = e.matmul(acc_out(bk, n), lhsT=wv[:, k, j * 128:(j + 1) * 128], rhs=hT_rhs(bi, slot, k, 0, n),
                                               start=(k == 0), stop=(k == 31))
                            return ins
                        CMP("pe", f_mm, r=["wres", hkey], w=["ps%d" % bk])
                        if pas == 1:
                            if j % 2 == 0:
                                CMP("act", lambda e, j=j, bk=bk, n=n, sg_=sg_: e.activation(out=sg_[:, j, 0:n], in_=bank(bk)[:, 0:n], func=AF.Copy),
                                    r=["ps%d" % bk], w=["stg%d" % slot])
                            else:
                                CMP("dve", lambda e, j=j, bk=bk, n=n, sg_=sg_: e.tensor_copy(out=sg_[:, j, 0:n], in_=bank(bk)[:, 0:n]),
                                    r=["ps%d" % bk], w=["stg%d" % slot])
                        else:
                            u = j % 2
                            rb = 4 + u
                            CMP("act", lambda e, bk=bk, n=n, u=u: e.activation(out=Tf[u][:, 0:n], in_=bank(bk)[:, 0:n], func=AF.Copy),
                                r=["ps%d" % bk], w=["Tf%d" % u])
                            if rope_pending[0] is not None:
                                rope_pending[0]()

                            def rope_rest(n=n, u=u, rb=rb, slot=slot, j=j, sg_=sg_, pas=pas, col0=col0):
                                CMP("pe", lambda e: e.matmul(bank(rb)[:, 0:n], lhsT=rot_f, rhs=Tf[u][:, 0:n], start=True, stop=True),
                                    r=["Tf%d" % u, "cst"], w=["ps%d" % rb])
                                CMP("dve", lambda e: e.tensor_tensor(out=t1[u][:, 0:n], in0=Tf[u][:, 0:n], in1=cosb[slot][:, 0:n], op=ALU.mult),
                                    r=["Tf%d" % u, "cs%d" % slot], w=["t1%d" % u])
                                CMP("dve", lambda e: e.tensor_tensor(out=t2[u][:, 0:n], in0=bank(rb)[:, 0:n], in1=sinb[slot][:, 0:n], op=ALU.mult),
                                    r=["ps%d" % rb, "cs%d" % slot], w=["t2%d" % u])
                                CMP("dve", lambda e: e.tensor_tensor(out=sg_[:, j, 0:n], in0=t1[u][:, 0:n], in1=t2[u][:, 0:n], op=ALU.add),
                                    r=["t1%d" % u, "t2%d" % u], w=["stg%d" % slot])
                                if j == 7:
                                    DMA("sp", QKT_v[:, pas * 8:(pas + 1) * 8, col0:col0 + n], sg_[:, :, 0:n], r=["stg%d" % slot], w=[newkey("QKT")])
                            rope_pending[0] = rope_rest
                    if pas == 1:
                        DMA("sp", QKT_v[:, pas * 8:(pas + 1) * 8, col0:col0 + n], sg_[:, :, 0:n], r=["stg%d" % slot], w=[newkey("QKT")])
                else:
                    nsub = 1 if tb < 0 else 4
                    for s_ in range(nsub):
                        np_ = 16 if tb < 0 else 128
                        tix = 0 if tb < 0 else 1 + tb * 4 + s_
                        row0 = 0 if tb < 0 else 16 + (tb * 4 + s_) * 128
                        vs = (bi * 4 + s_) % 2

                        b0 = 3 * vs

                        def f_mm(e, bi=bi, slot=slot, s_=s_, np_=np_, wv=wv, b0=b0):
                            ins = None
                            for k in range(32):
                                lt = hT_rhs(bi, slot, k, s_ * 128 if np_ == 128 else 0, np_)
                                e.matmul(bank(b0)[:np_, :], lhsT=lt, rhs=wv[:, k, 0:512], start=(k == 0), stop=(k == 31))
                                e.matmul(bank(b0 + 1)[:np_, :], lhsT=lt, rhs=wv[:, k, 512:1024], start=(k == 0), stop=(k == 31))
                                ins = e.matmul(bank(b0 + 2)[:np_, 0:4], lhsT=lt, rhs=wv[:, k, 1024:1028], start=(k == 0), stop=(k == 31))
                            return ins
                        CMP("pe", f_mm, r=["wres", hkey], w=["ps%d" % b0, "ps%d" % (b0 + 1), "ps%d" % (b0 + 2)])
                        CMP("act", lambda e, np_=np_, vs=vs, b0=b0: e.activation(out=vst[vs][:np_, 0:512], in_=bank(b0)[:np_, :], func=AF.Copy),
                            r=["ps%d" % b0], w=["vst%d" % vs])
                        CMP("dve", lambda e, np_=np_, vs=vs, b0=b0: e.tensor_copy(out=vst[vs][:np_, 512:1024], in_=bank(b0 + 1)[:np_, :]),
                            r=["ps%d" % (b0 + 1)], w=["vst%d" % vs])
                        CMP("dve", lambda e, np_=np_, tix=tix, b0=b0: e.tensor_copy(out=Zt[:np_, :, tix], in_=bank(b0 + 2)[:np_, 0:4]),
                            r=["ps%d" % (b0 + 2)], w=["Zt"])
                        DMA("sp", Vd[row0:row0 + np_, :], vst[vs][:np_, :], r=["vst%d" % vs], w=[newkey("Vd")])

        stop("p2", [("b", 0, QKT[0:128, 0:512], allkeys("QKT")),
                    ("b", 1, QKT[9 * 128:10 * 128, 4096:4608], allkeys("QKT")),
                    ("b", 2, Vd[0:128, 0:512], allkeys("Vd")),
                    ("b", 3, Vd[4000:4128, 512:1024], allkeys("Vd"))])
        for h in range(4):
            CMP("dve", lambda e, h=h: e.tensor_scalar(out=Zt[:, h, :], in0=Zt[:, h, :], scalar1=bfort[:, h:h + 1], scalar2=None, op0=ALU.add),
                r=["Zt", "bfort"], w=["Zt"])
        Zf = Zt.rearrange("p h t -> p (h t)")
        Cf = Ct.rearrange("p h t -> p (h t)")
        CMP("act", lambda e: e.activation(out=Zf, in_=Zf, func=AF.Exp, scale=-1.0), r=["Zt"], w=["Zt"])
        CMP("act", lambda e: e.activation(out=Zf, in_=Zf, func=AF.Ln, bias=onet, scale=1.0), r=["Zt", "cst"], w=["Zt"])
        CMP("dve", lambda e: e.tensor_scalar(out=Zt[:, :, 0], in0=Zt[:, :, 0], scalar1=mask16, scalar2=None, op0=ALU.mult),
            r=["Zt", "cst"], w=["Zt"])
        CMP("pe", lambda e: e.matmul(bank(0)[:, 0:260], lhsT=tri_f, rhs=Zf, start=True, stop=True), r=["Zt", "cst"], w=["ps0"])
        CMP("pe", lambda e: e.matmul(bank(1)[:, 0:260], lhsT=ones_f, rhs=Zf, start=True, stop=True), r=["Zt", "cst"], w=["ps1"])
        TOT = CREF
        CMP("dve", lambda e: e.tensor_copy(out=TOT.rearrange("p h t -> p (h t)"), in_=bank(1)[:, 0:260]), r=["ps1"], w=["CREF"])
        PX = Zt
        onesrow = ones_f[:, 0:65]
        for h in range(4):
            CMP("dve", lambda e, h=h: e.tensor_tensor_scan(out=PX[:, h, :], data0=onesrow, data1=TOT[:, h, :], initial=0.0,
                                                           op0=ALU.mult, op1=ALU.add), r=["CREF", "Zt", "cst", "ps0", "ps1"], w=["Zt"])
        CMP("dve", lambda e: e.tensor_copy(out=Cf, in_=bank(0)[:, 0:260]), r=["ps0"], w=["Ct"])
        CMP("dve", lambda e: e.tensor_tensor(out=Ct[:, :, 1:65], in0=Ct[:, :, 1:65], in1=PX[:, :, 0:64], op=ALU.add), r=["Ct", "Zt"], w=["Ct"])
        CMP("pe", lambda e: e.matmul(bank(2)[:, 0:260], lhsT=e0_f, rhs=Cf, start=True, stop=True), r=["Ct", "cst"], w=["ps2"])
        CMP("dve", lambda e: e.tensor_copy(out=CREF.rearrange("p h t -> p (h t)"), in_=bank(2)[:, 0:260]), r=["ps2", "Zt"], w=["CREF"])

        al.off = PERS
        QK = al.bf16(4 * L).rearrange("p (m t) -> p m t", t=L)
        Vs = al.bf16(65 * 256)
        PT = [al.bf16(512) for _ in range(6)]
        ob = [al.bf16(2 * 512).rearrange("p (c t) -> p c t", t=512) for _ in range(2)]
        BT = al.f32(64 * 65).rearrange("p (q k) -> p q k", k=65)
        rec = [al.f32(512) for _ in range(2)]
        o1n = al.f32(1024).rearrange("p (c t) -> p c t", t=512)
        o2n = al.f32(1024).rearrange("p (c t) -> p c t", t=512)
        dd = al.f32(1024).rearrange("p (c t) -> p c t", t=512)
        dsq = al.f32(1024).rearrange("p (c t) -> p c t", t=512)
        rsb = al.f32(512)
        abias = al.f32(65)
        p2keys = ["wres", "hTb0", "hTb1", "stg0", "stg1", "vst0", "vst1", "Tf0", "Tf1", "t10", "t11", "t20", "t21", "cs0", "cs1"]
        barrier(p2keys, ["QK", "Vs", "BT"] + ["pt%d" % i for i in range(6)] + ["ob0", "ob1", "rec0", "rec1", "o1n", "o2n", "dd", "dsq", "rsb", "abias"])
        pt_ctr = [0]
        ob_ctr = [0]
        last_attn = [None]

        def load_head(chunks, vc0, dv):
            for m, ch in enumerate(chunks):
                DMA("sp", QK[:, m, :], QKT[ch * 128:(ch + 1) * 128, :], r=allkeys("QKT"), w=["QK"])
            Vv = Vs[:, 0:65 * dv].rearrange("p (t c) -> p t c", c=dv)
            DMA("sp", Vv[0:16, 0, :], Vd[0:16, vc0:vc0 + dv], r=allkeys("Vd"), w=["Vs"])
            Vsrc = Vd[16:L, vc0:vc0 + dv].rearrange("(t p) c -> p t c", p=128)
            for g in range(4):
                DMA("sp", Vv[:, 1 + g * 16:1 + (g + 1) * 16, :], Vsrc[:, g * 16:(g + 1) * 16, :], r=allkeys("Vd"), w=["Vs"])
            return Vv

        for hh in range(2):
            Vv = load_head([4 * hh, 4 * hh + 1, 4 * hh + 2, 4 * hh + 3], 256 * hh, 256)
            for J in range(16):
                qc0 = 16 + 512 * J
                ktiles = [-1] + list(range(4 * J + 4))
                pend = []
                for kt in ktiles:
                    kp = 16 if kt < 0 else 128
                    kc0 = 0 if kt < 0 else 16 + 128 * kt
                    i = -1 if kt < 0 else kt - 4 * J
                    c0 = max(i, 0) * 128
                    first = kt < 0
                    last = kt == 4 * J + 3
                    cur = []
                    for m in range(2):
                        sb = m
                        slot = pt_ctr[0] % 6
                        pt_ctr[0] += 1
                        CMP("pe", lambda e, sb=sb, kp=kp, c0=c0, kc0=kc0, m=m, qc0=qc0: e.matmul(
                            bank(sb)[:kp, c0:512], lhsT=QK[:, 2 + m, kc0:kc0 + kp], rhs=QK[:, m, qc0 + c0:qc0 + 512], start=True, stop=True),
                            r=["QK"], w=["ps%d" % sb])
                        CMP("act", lambda e, sb=sb, kp=kp, c0=c0, slot=slot: e.activation(
                            out=PT[slot][:kp, c0:512], in_=bank(sb)[:kp, c0:512], func=AF.Exp, scale=SCALE),
                            r=["ps%d" % sb], w=["pt%d" % slot])
                        if i >= 0:
                            CMP("dve", lambda e, slot=slot, c0=c0: e.tensor_tensor(
                                out=PT[slot][:, c0:c0 + 128], in0=PT[slot][:, c0:c0 + 128], in1=maskdb, op=ALU.mult),
                                r=["pt%d" % slot, "maskdb"], w=["pt%d" % slot])

                        def f_av(e, m=m, kp=kp, c0=c0, slot=slot, kt=kt, first=first, last=last, Vv=Vv):
                            for c in range(2):
                                e.matmul(bank(2 + 2 * m + c)[:, c0:512], lhsT=Vv[:kp, kt + 1, c * 128:(c + 1) * 128],
                                         rhs=PT[slot][:kp, c0:512], start=first, stop=last)
                            return e.matmul(bank(6 + m)[:, c0:512], lhsT=onesb[:kp, :], rhs=PT[slot][:kp, c0:512], start=first, stop=last)
                        cur.append((f_av, ["pt%d" % slot, "Vs", "onesb"], ["ps%d" % (2 + 2 * m), "ps%d" % (3 + 2 * m), "ps%d" % (6 + m)]))
                    for (fn_, r_, w_) in pend:
                        CMP("pe", fn_, r=r_, w=w_)
                    pend = cur
                for (fn_, r_, w_) in pend:
                    CMP("pe", fn_, r=r_, w=w_)
                osl = ob_ctr[0] % 2
                drip(1, [last_attn[0]] if last_attn[0] else [])
                ob_ctr[0] += 1
                for m_ in range(2):
                    CMP("act", lambda e, m_=m_: e.activation(out=rec[m_], in_=bank(6 + m_), func=AF.Ln), r=["ps%d" % (6 + m_)], w=["rec%d" % m_])
                    CMP("act", lambda e, m_=m_: e.activation(out=rec[m_], in_=rec[m_], func=AF.Exp, scale=-1.0), r=["rec%d" % m_], w=["rec%d" % m_])
                for c in range(2):
                    CMP("dve", lambda e, c=c: e.tensor_tensor(out=o1n[:, c, :], in0=bank(2 + c), in1=rec[0], op=ALU.mult),
                        r=["ps%d" % (2 + c), "rec0"], w=["o1n"])
                    CMP("dve", lambda e, c=c: e.tensor_tensor(out=o2n[:, c, :], in0=bank(4 + c), in1=rec[1], op=ALU.mult),
                        r=["ps%d" % (4 + c), "rec1"], w=["o2n"])
                ddf = dd.rearrange("p c t -> p (c t)")
                CMP("dve", lambda e, ddf=ddf: e.scalar_tensor_tensor(out=ddf, in0=o2n.rearrange("p c t -> p (c t)"), scalar=nlam,
                                                                     in1=o1n.rearrange("p c t -> p (c t)"), op0=ALU.mult, op1=ALU.add),
                    r=["o1n", "o2n", "nlam"], w=["dd"])
                CMP("act", lambda e, ddf=ddf: e.activation(out=dsq.rearrange("p c t -> p (c t)"), in_=ddf, func=AF.Square), r=["dd"], w=["dsq"])

                def f_ss(e):
                    e.matmul(bank(6), lhsT=ones_f, rhs=dsq[:, 0, :], start=True, stop=False)
                    return e.matmul(bank(6), lhsT=ones_f, rhs=dsq[:, 1, :], start=False, stop=True)
                CMP("pe", f_ss, r=["dsq", "cst"], w=["ps6"])
                CMP("act", lambda e: e.activation(out=rsb, in_=bank(6), func=AF.Ln, bias=epst, scale=1.0 / 256.0), r=["ps6", "cst"], w=["rsb"])
                CMP("act", lambda e: e.activation(out=rsb, in_=rsb, func=AF.Exp, scale=-0.5), r=["rsb"], w=["rsb"])
                for c in range(2):
                    CMP("dve", lambda e, c=c: e.tensor_tensor(out=dd[:, c, :], in0=dd[:, c, :], in1=rsb, op=ALU.mult), r=["dd", "rsb"], w=["dd"])
                    CMP("dve", lambda e, c=c, osl=osl: e.tensor_scalar(out=ob[osl][:, c, :], in0=dd[:, c, :], scalar1=sgt[:, c:c + 1],
                                                                       scalar2=1.0 - LAMBDA_INIT, op0=ALU.mult, op1=ALU.mult),
                        r=["dd", "sgt"], w=["ob%d" % osl])
                last_attn[0] = newkey("attn_own%d" % J)
                DMA("sp", attn_own_q[J][256 * hh:256 * hh + 256, :].rearrange("(c p) t -> p c t", p=128), ob[osl],
                    r=["ob%d" % osl], w=[last_attn[0]])

        for fh in range(4):
            Vv = load_head([8 + 2 * fh, 9 + 2 * fh], 512 + 128 * fh, 128)
            CMP("dve", lambda e, fh=fh: e.tensor_tensor(out=abias, in0=Ct[:, fh, :], in1=CREF[:, fh, :], op=ALU.subtract),
                r=["Ct", "CREF"], w=["abias"])
            for qt in range(64):
                CMP("dve", lambda e, qt=qt, fh=fh: e.tensor_scalar(out=BT[:, qt, :], in0=CREF[:, fh, :], scalar1=CREF[:, fh, qt + 1:qt + 2],
                                                                   scalar2=None, op0=ALU.subtract), r=["CREF"], w=["BT"])
            BTf = BT.rearrange("p q k -> p (q k)")
            CMP("act", lambda e, BTf=BTf: e.activation(out=BTf, in_=BTf, func=AF.Exp), r=["BT"], w=["BT"])
            for J in range(16):
                qc0 = 16 + 512 * J
                par = J % 2
                bo = 2 + par
                bl = 4 + par
                ktiles = [-1] + list(range(4 * J + 4))
                pend = []
                for kt in ktiles:
                    kp = 16 if kt < 0 else 128
                    kc0 = 0 if kt < 0 else 16 + 128 * kt
                    i = -1 if kt < 0 else kt - 4 * J
                    c0 = max(i, 0) * 128
                    first = kt < 0
                    last = kt == 4 * J + 3
                    sb = pt_ctr[0] % 2
                    slot = pt_ctr[0] % 6
                    pt_ctr[0] += 1
                    CMP("pe", lambda e, sb=sb, kp=kp, c0=c0, kc0=kc0, qc0=qc0: e.matmul(
                        bank(sb)[:kp, c0:512], lhsT=QK[:, 1, kc0:kc0 + kp], rhs=QK[:, 0, qc0 + c0:qc0 + 512], start=True, stop=True),
                        r=["QK"], w=["ps%d" % sb])

                    CMP("act", lambda e, sb=sb, kp=kp, c0=c0, slot=slot, kt=kt: e.activation(
                        out=PT[slot][:kp, c0:512], in_=bank(sb)[:kp, c0:512], func=AF.Exp, bias=abias[:kp, kt + 1:kt + 2], scale=SCALE),
                        r=["ps%d" % sb, "abias"], w=["pt%d" % slot])

                    def f_scale(e, kp=kp, c0=c0, slot=slot, kt=kt, J=J):
                        q0 = c0 // 128
                        nq = 4 - q0
                        pv = PT[slot][:kp, c0:512].rearrange("p (q t) -> p q t", t=128)
                        wv_ = BT[:kp, 4 * J + q0:4 * J + 4, kt + 1].unsqueeze(2).to_broadcast([kp, nq, 128])
                        return e.tensor_tensor(out=pv, in0=pv, in1=wv_, op=ALU.mult)
                    sc_eng = "pool" if (i < 0 and kt >= 0 and pt_ctr[0] % 3 == 2) else "dve"
                    CMP(sc_eng, f_scale, r=["pt%d" % slot, "BT"], w=["pt%d" % slot])
                    if i >= 0:
                        CMP("dve", lambda e, slot=slot, c0=c0: e.tensor_tensor(
                            out=PT[slot][:, c0:c0 + 128], in0=PT[slot][:, c0:c0 + 128], in1=trib, op=ALU.mult),
                            r=["pt%d" % slot, "trib"], w=["pt%d" % slot])

                    def f_av(e, kp=kp, c0=c0, slot=slot, kt=kt, first=first, last=last, Vv=Vv, bo=bo, bl=bl):
                        e.matmul(bank(bo)[:, c0:512], lhsT=Vv[:kp, kt + 1, :], rhs=PT[slot][:kp, c0:512], start=first, stop=last)
                        return e.matmul(bank(bl)[:, c0:512], lhsT=onesb[:kp, :], rhs=PT[slot][:kp, c0:512], start=first, stop=last)
                    if len(pend) == 2:
                        fn_, r_, w_ = pend.pop(0)
                        CMP("pe", fn_, r=r_, w=w_)
                    pend.append((f_av, ["pt%d" % slot, "Vs", "onesb"], ["ps%d" % bo, "ps%d" % bl]))
                for (fn_, r_, w_) in pend:
                    CMP("pe", fn_, r=r_, w=w_)
                osl = ob_ctr[0] % 2
                drip(1, [last_attn[0]] if last_attn[0] else [])
                ob_ctr[0] += 1
                CMP("dve", lambda e, par=par, bl=bl: e.reciprocal(out=rec[par], in_=bank(bl)), r=["ps%d" % bl], w=["rec%d" % par])
                CMP("dve", lambda e, par=par, bo=bo, osl=osl: e.tensor_tensor(out=ob[osl][:, 0, :], in0=bank(bo), in1=rec[par], op=ALU.mult),
                    r=["ps%d" % bo, "rec%d" % par], w=["ob%d" % osl])
                last_attn[0] = newkey("attn_own%d" % J)
                DMA("sp", attn_own_q[J][512 + 128 * fh:512 + 128 * fh + 128, :], ob[osl][:, 0, :],
                    r=["ob%d" % osl], w=[last_attn[0]])
                if fh == 3:
                    allgather(attn_own_q[J], attn_all_q[J], G4, allkeys("attn_own%d" % J), "attn_all%d" % J)

        drip(10000)
        all_attn = [k for J in range(16) for k in allkeys("attn_own%d" % J)]
        stop("p3", [("b", 0, attn_own_q[0][0:128, :], all_attn),
                    ("b", 1, attn_own_q[9][384:512, :], all_attn),
                    ("b", 2, attn_own_q[0][512:640, :], all_attn),
                    ("b", 3, attn_own_q[15][896:1024, :], all_attn)])

        al.off = PERS
        actT = al.bf16(128 * 256).rearrange("p (f t) -> p f t", t=256)
        atT = actT.rearrange("p f t -> p (f t)")[:, 0:32 * 256].rearrange("p (k t) -> p k t", t=256)
        ybM = actT.rearrange("p f t -> p (f t)")[:, 32 * 256:32 * 256 + D]
        h1T = al.bf16(32 * 256).rearrange("p (k t) -> p k t", t=256)
        ring = [al.bf16(32 * 256) for _ in range(3)]
        ytm = [al.f32(D) for _ in range(2)]
        Gm = al.f32(D)
        Bm = al.f32(D)
        rbuf = [al.f32(512) for _ in range(2)]
        p3keys = ["QK", "Vs", "BT", "abias"] + ["pt%d" % i for i in range(6)] + ["ob0", "ob1", "rec0", "rec1", "o1n", "o2n", "dd", "dsq", "rsb"]
        barrier(p3keys, ["actT", "h1T", "ring0", "ring1", "ring2", "ytm0", "ytm1", "gbm", "rbuf0", "rbuf1"])
        w_out_v = w_out_b.rearrange("(k p) c -> p k c", p=128)
        w_up_v = w_up_b.rearrange("(k p) c -> p k c", p=128)
        w_down_v = w_down_b.rearrange("(f p) c -> p f c", p=128)
        attn_v = [t.rearrange("(k p) t -> p k t", p=128) for t in attn_all_q]
        loads = []
        for blk in range(8):
            for cg in range(4):
                for kq in range(4):
                    loads.append((w_out_v[:, kq * 8:(kq + 1) * 8, cg * 1024:(cg + 1) * 1024], "w_out_b"))
            for fg in range(16):
                for kq in range(4):
                    loads.append((w_up_v[:, kq * 8:(kq + 1) * 8, fg * 1024:(fg + 1) * 1024], "w_up_b"))
            for cg in range(4):
                for pc in range(16):
                    loads.append((w_down_v[:, pc * 8:(pc + 1) * 8, cg * 1024:(cg + 1) * 1024], "w_down_b"))
        issued = [0]

        def slot_view(i):
            return ring[i % 3].rearrange("p (k c) -> p k c", c=1024)

        def need(i):
            while issued[0] <= min(i + 2, len(loads) - 1):
                j = issued[0]
                DMA("sp", slot_view(j), loads[j][0], r=allkeys(loads[j][1]), w=["ring%d" % (j % 3)])
                issued[0] += 1

        li = 0
        for blk in range(8):
            need(li)
            for s_ in range(2):
                def f_at(e, blk=blk, s_=s_):
                    return e.dma_start(out=atT[:, :, s_ * 128:(s_ + 1) * 128], in_=attn_v[2 * blk + s_][:, :, bass.ds(regs["off"], 128)])
                S.op("sp", f_at, reads=["attn_all%d" % (2 * blk + s_)], writes=["actT"], kind="d")
            for s_ in range(2):
                DMA("sp", ytm[s_], h_own[blk * 256 + s_ * 128:blk * 256 + (s_ + 1) * 128, :], r=allkeys("h_own"), w=["ytm%d" % s_])
            DMA("sp", Gm, lnrow[2:3, :].partition_broadcast(128), w=["gbm"])
            DMA("sp", Bm, lnrow[3:4, :].partition_broadcast(128), w=["gbm"])
            for cg in range(4):
                for kq in range(4):
                    need(li)
                    wv = slot_view(li)
                    rk = "ring%d" % (li % 3)
                    li += 1
                    for s_ in range(2):
                        for hf in range(2):
                            bk = 4 * (cg % 2) + 2 * s_ + hf

                            def f_mm(e, wv=wv, s_=s_, hf=hf, bk=bk, kq=kq):
                                ins = None
                                for kk in range(8):
                                    ins = e.matmul(bank(bk), lhsT=atT[:, kq * 8 + kk, s_ * 128:(s_ + 1) * 128],
                                                   rhs=wv[:, kk, hf * 512:(hf + 1) * 512],
                                                   start=(kq == 0 and kk == 0), stop=(kq == 3 and kk == 7))
                                return ins
                            CMP("pe", f_mm, r=["actT", rk], w=["ps%d" % bk])
                for s_ in range(2):
                    for hf in range(2):
                        bk = 4 * (cg % 2) + 2 * s_ + hf
                        c0 = cg * 1024 + hf * 512
                        CMP("dve", lambda e, s_=s_, bk=bk, c0=c0: e.scalar_tensor_tensor(
                            out=ytm[s_][:, c0:c0 + 512], in0=ytm[s_][:, c0:c0 + 512], scalar=ALPHA,
                            in1=bank(bk), op0=ALU.mult, op1=ALU.add), r=["ps%d" % bk, "ytm%d" % s_], w=["ytm%d" % s_])
            for s_ in range(2):
                layer_norm(ytm[s_], 128, "ytm%d" % s_, Gm, Bm, "gbm", yb=ybM, ybkey="actT",
                           hT_dst=h1T[:, :, s_ * 128:(s_ + 1) * 128], hTkey="h1T")
            DMA("sp", Gm, lnrow[4:5, :].partition_broadcast(128), w=["gbm"])
            DMA("sp", Bm, lnrow[5:6, :].partition_broadcast(128), w=["gbm"])
            for fg in range(16):
                b0 = 4 * (fg % 2)
                for kq in range(4):
                    need(li)
                    wv = slot_view(li)
                    rk = "ring%d" % (li % 3)
                    li += 1
                    for f8 in range(8):
                        bk = b0 + f8 // 2
                        cc0 = (f8 % 2) * 256

                        def f_mm(e, wv=wv, f8=f8, bk=bk, cc0=cc0, kq=kq):
                            ins = None
                            for kk in range(8):
                                ins = e.matmul(bank(bk)[:, cc0:cc0 + 256], lhsT=wv[:, kk, f8 * 128:(f8 + 1) * 128],
                                               rhs=h1T[:, kq * 8 + kk, :], start=(kq == 0 and kk == 0 and f8 % 2 == 0),
                                               stop=(kq == 3 and kk == 7), skip_group_check=True)
                            return ins
                        CMP("pe", f_mm, r=["h1T", rk], w=["ps%d" % bk])
                for pr in range(4):
                    bk = b0 + pr
                    rb = pr % 2
                    f = fg * 8 + 2 * pr
                    CMP("act", lambda e, bk=bk, rb=rb: e.activation(out=rbuf[rb], in_=bank(bk), func=AF.Relu),
                        r=["ps%d" % bk], w=["rbuf%d" % rb])
                    CMP("pool", lambda e, rb=rb, f=f: e.tensor_tensor(out=actT[:, f:f + 2, :].rearrange("p f t -> p (f t)"), in0=rbuf[rb], in1=rbuf[rb], op=ALU.mult),
                        r=["rbuf%d" % rb], w=["actT"])
            for cg in range(4):
                for pc in range(16):
                    need(li)
                    wv = slot_view(li)
                    rk = "ring%d" % (li % 3)
                    li += 1
                    for s_ in range(2):
                        for hf in range(2):
                            bk = 4 * (cg % 2) + 2 * s_ + hf

                            def f_mm(e, wv=wv, s_=s_, hf=hf, bk=bk, pc=pc):
                                ins = None
                                for kk in range(8):
                                    ins = e.matmul(bank(bk), lhsT=actT[:, pc * 8 + kk, s_ * 128:(s_ + 1) * 128],
                                                   rhs=wv[:, kk, hf * 512:(hf + 1) * 512],
                                                   start=(pc == 0 and kk == 0), stop=(pc == 15 and kk == 7))
                                return ins
                            CMP("pe", f_mm, r=["actT", rk], w=["ps%d" % bk])
                for s_ in range(2):
                    for hf in range(2):
                        bk = 4 * (cg % 2) + 2 * s_ + hf
                        c0 = cg * 1024 + hf * 512
                        CMP("dve", lambda e, s_=s_, bk=bk, c0=c0: e.scalar_tensor_tensor(
                            out=ytm[s_][:, c0:c0 + 512], in0=ytm[s_][:, c0:c0 + 512], scalar=ALPHA,
                            in1=bank(bk), op0=ALU.mult, op1=ALU.add), r=["ps%d" % bk, "ytm%d" % s_], w=["ytm%d" % s_])
            for s_ in range(2):
                layer_norm(ytm[s_], 128, "ytm%d" % s_, Gm, Bm, "gbm")
                DMA("sp", out[blk * 256 + s_ * 128:blk * 256 + (s_ + 1) * 128, :], ytm[s_], r=["ytm%d" % s_], w=[newkey("out")])
        S.op("sp", None, reads=allkeys("out"), kind="w")
        S.emit(nc)


_NC_CACHE = {}


def _consts():
    c = np.zeros((128, 7, 128), np.float32)
    c[:, 0, :] = np.eye(128, dtype=np.float32)
    c[:, 1, :] = np.triu(np.ones((128, 128), np.float32))
    rot = np.zeros((128, 128), np.float32)
    for m in range(64):
        rot[m + 64, m] = -1.0
        rot[m, m + 64] = 1.0
    c[:, 2, :] = rot
    c[0, 3, :] = 1.0
    md = np.ones((128, 128), np.float32)
    md[64:, :64] = 0.0
    c[:, 4, :] = md
    c[:, 5, :] = 1.0
    c[:16, 6, 0] = 1.0
    c[:, 6, 1] = EPS
    c[:, 6, 2] = 1.0
    return np.ascontiguousarray(c.reshape(128, 7 * 128))


def _rope():
    inv = (1.0 / (np.float32(10000.0) ** (np.arange(0, HD, 2, dtype=np.float32) / np.float32(HD)))).astype(np.float32)
    ang = (np.arange(L, dtype=np.float32)[:, None] * inv[None, :]).astype(np.float32)
    ang = np.concatenate([ang, ang], axis=-1)
    return np.ascontiguousarray(np.cos(ang).T.astype(np.float32)), np.ascontiguousarray(np.sin(ang).T.astype(np.float32))


def _prep(x, meta_tokens, ln_in_g, ln_in_b, w_in, b_forget, lambda_q1, lambda_k1, lambda_q2, lambda_k2,
          subln_g, w_out, ln_attn_g, ln_attn_b, w_up, w_down, ln_mlp_g, ln_mlp_b):
    f = lambda a: np.asarray(a, dtype=np.float32)
    x = f(x)
    w_in0 = f(w_in)[0]
    w_out0 = f(w_out)[0]
    w_up0 = f(w_up)[0]
    w_down0 = f(w_down)[0]
    lnrow = np.ascontiguousarray(np.stack([f(ln_in_g), f(ln_in_b), f(ln_attn_g)[0], f(ln_attn_b)[0], f(ln_mlp_g)[0], f(ln_mlp_b)[0]]))
    lamv = np.ascontiguousarray(np.stack([f(lambda_q1)[0], f(lambda_k1)[0], f(lambda_q2)[0], f(lambda_k2)[0]]))
    sgv = np.ascontiguousarray(f(subln_g)[0].reshape(2, 128).T)
    consts = _consts()
    cosv, sinv = _rope()
    rows = []
    for rr in range(4):
        for h in (2 * rr, 2 * rr + 1):
            rows.extend(range(h * 256, (h + 1) * 256))
        for fh in range(4 * rr, 4 * rr + 4):
            rows.extend(range(2048 + fh * 128, 2048 + (fh + 1) * 128))
    w_out_p = np.ascontiguousarray(w_out0[np.array(rows)])
    w_up0 = np.ascontiguousarray(w_up0)
    w_down0 = np.ascontiguousarray(w_down0)
    in_maps = []
    for c in range(8):
        b, r = c // 4, c % 4
        cols = []
        for h in (2 * r, 2 * r + 1):
            for off in (DQ_OFF, DK_OFF):
                for m in range(2):
                    cols.extend(range(off + (2 * h + m) * 128, off + (2 * h + m + 1) * 128))
        for fh in range(4 * r, 4 * r + 4):
            cols.extend(range(FQ_OFF + fh * 128, FQ_OFF + (fh + 1) * 128))
            cols.extend(range(FK_OFF + fh * 128, FK_OFF + (fh + 1) * 128))
        for h in (2 * r, 2 * r + 1):
            cols.extend(range(DV_OFF + h * 256, DV_OFF + (h + 1) * 256))
        for fh in range(4 * r, 4 * r + 4):
            cols.extend(range(FV_OFF + fh * 128, FV_OFF + (fh + 1) * 128))
        for fh in range(4 * r, 4 * r + 4):
            cols.append(FF_OFF + fh)
        assert len(cols) == NCOL
        in_maps.append({
            "x": np.ascontiguousarray(x[b].reshape(16, 4, 128, D)[:, r].reshape(2048, D)),
            "meta": f(meta_tokens),
            "lnrow": lnrow,
            "w_in": np.ascontiguousarray(w_in0[:, np.array(cols)]),
            "w_out_f": w_out_p,
            "w_up_f": w_up0,
            "w_down_f": w_down0,
            "bfor": np.ascontiguousarray(np.broadcast_to(f(b_forget)[0][4 * r:4 * r + 4][None, :], (128, 4))),
            "lam": lamv,
            "sg": sgv,
            "consts": consts,
            "cos": cosv,
            "sin": sinv,
            "tok0": np.array([[128 * r]], np.int32),
        })
    return in_maps


def kernel(**inputs):
    in_maps = _prep(**inputs)
    if "nc" not in _NC_CACHE:
        _NC_CACHE["nc"] = build_nc()
    nc = _NC_CACHE["nc"]
    res = run_bass_kernel_spmd(nc, in_maps, core_ids=list(range(8)))
    outp = np.empty((2, SEQ, D), np.float32)
    for c in range(8):
        b, r = c // 4, c % 4
        outp[b].reshape(16, 4, 128, D)[:, r] = np.asarray(res.results[c]["out"], dtype=np.float32).reshape(16, 128, D)
    return outp
```

```python
import contextlib
import math
import numpy as np
import concourse.bass as bass
import concourse.mybir as mybir
from concourse.bass_utils import run_bass_kernel_spmd

F32 = mybir.dt.float32
BF16 = mybir.dt.bfloat16
AF = mybir.ActivationFunctionType
ALU = mybir.AluOpType
AX = mybir.AxisListType

D = 4096
SEQ = 8192
NMETA = 16
L = SEQ + NMETA
HD = 128
DFF = 16384
NCOL = 3076
ALPHA = 2.0 ** 0.25
LAMBDA_INIT = 0.8 - 0.6 * math.exp(0.0)
SCALE = HD ** -0.5
EPS = 1e-5
DQ_OFF = 0
DK_OFF = 2048
DV_OFF = 4096
FQ_OFF = 6144
FK_OFF = 8192
FV_OFF = 10240
FF_OFF = 12288


class Sched:
    ENGINES = ("pe", "act", "dve", "pool", "sp")
    NDSEM = 8

    def __init__(self):
        self.ops = []
        self.last_writer = {}
        self.readers = {}
        self.dma_count = {e: 0 for e in self.ENGINES}
        self.dma_hist = {e: [] for e in self.ENGINES}
        self.ncc = 0
        self.sp_prologue = None

    def op(self, eng, fn, reads=(), writes=(), kind="c"):
        oid = len(self.ops)
        deps = set()
        for r in reads:
            w = self.last_writer.get(r)
            if w is not None:
                deps.add(w)
        for w_ in writes:
            w = self.last_writer.get(w_)
            if w is not None:
                deps.add(w)
            for rd in self.readers.get(w_, ()):
                deps.add(rd)
        rec = dict(id=oid, eng=eng, fn=fn, kind=kind, deps=deps, signal=False, ticket=None)
        if kind == "d":
            j = self.dma_count[eng]
            self.dma_count[eng] += 1
            hist = self.dma_hist[eng]
            if j >= self.NDSEM:
                deps.add(hist[j - self.NDSEM])
            hist.append(oid)
            rec["dma_idx"] = j
            rec["signal"] = True
        if kind == "cc":
            rec["cc_idx"] = self.ncc
            self.ncc += 1
            rec["signal"] = True
        deps.discard(oid)
        self.ops.append(rec)
        for r in reads:
            self.readers.setdefault(r, []).append(oid)
        for w_ in writes:
            self.last_writer[w_] = oid
            self.readers[w_] = []
        return oid

    def emit(self, nc):
        ops = self.ops
        for o in ops:
            for d in o["deps"]:
                Dp = ops[d]
                if Dp["kind"] == "c":
                    if Dp["eng"] == "pe" and o["eng"] == "pe" and o["kind"] == "c":
                        continue
                    Dp["signal"] = True
        with contextlib.ExitStack() as es:
            esem = {e: es.enter_context(nc.semaphore("es_" + e)) for e in self.ENGINES}
            dsem = {e: [es.enter_context(nc.semaphore("ds_%s_%d" % (e, i))) for i in range(self.NDSEM)]
                    for e in self.ENGINES if self.dma_count[e] > 0}
            ccsem = [es.enter_context(nc.semaphore("cc_%d" % i)) for i in range(self.ncc)]
            cnt = {e: 0 for e in self.ENGINES}
            for o in ops:
                if o["kind"] == "c" and o["signal"]:
                    cnt[o["eng"]] += 1
                    o["ticket"] = (esem[o["eng"]], cnt[o["eng"]], "e_" + o["eng"])
                elif o["kind"] == "d":
                    j = o["dma_idx"]
                    o["ticket"] = (dsem[o["eng"]][j % self.NDSEM], 16 * (j // self.NDSEM + 1),
                                   "d_%s_%d" % (o["eng"], j % self.NDSEM))
                elif o["kind"] == "cc":
                    o["ticket"] = (ccsem[o["cc_idx"]], 1, "cc_%d" % o["cc_idx"])
            block = es.enter_context(nc.Block())
            per_eng = {e: [o for o in ops if o["eng"] == e] for e in self.ENGINES}

            def run(engname, eng):
                waited = {}
                for o in per_eng[engname]:
                    for d in sorted(o["deps"]):
                        Dp = ops[d]
                        if Dp["ticket"] is None:
                            continue
                        if Dp["kind"] == "c" and Dp["eng"] == "pe" and engname == "pe" and o["kind"] == "c":
                            continue
                        sem, val, nm = Dp["ticket"]
                        if waited.get(nm, 0) >= val:
                            continue
                        eng.wait_ge(sem, val)
                        waited[nm] = val
                    if o["kind"] == "w":
                        continue
                    ins = o["fn"](eng)
                    if o["kind"] == "d":
                        ins.then_inc(o["ticket"][0], 16)
                    elif o["kind"] == "cc":
                        ins.then_inc(o["ticket"][0])
                    elif o["signal"]:
                        ins.then_inc(o["ticket"][0], 1)

            @block.tensor
            def _(e):
                run("pe", e)

            @block.scalar
            def _(e):
                run("act", e)

            @block.vector
            def _(e):
                run("dve", e)

            @block.gpsimd
            def _(e):
                run("pool", e)

            @block.sync
            def _(e):
                with contextlib.ExitStack() as es2:
                    if self.sp_prologue is not None:
                        self.sp_prologue(e, es2)
                    run("sp", e)


class _Stop(Exception):
    pass


def build_nc(stop_after=None):
    nc = bass.Bass("TRN2", target_bir_lowering=False)
    try:
        _build_body(nc, stop_after)
    except _Stop:
        pass
    return nc


def _build_body(nc, stop_after):

    def din(name, shape, dt=F32):
        return nc.dram_tensor(name, shape, dt, kind="ExternalInput").ap()

    def dint(name, shape, dt):
        return nc.dram_tensor(name, shape, dt).ap()

    x = din("x", [2048, D])
    meta = din("meta", [NMETA, D])
    lnrow = din("lnrow", [6, D])
    w_in = din("w_in", [D, NCOL])
    big = stop_after is None or stop_after in ("p4",)
    if big:
        w_out_f = din("w_out_f", [D, D])
        w_up_f = din("w_up_f", [D, DFF])
        w_down_f = din("w_down_f", [DFF, D])
    bfor = din("bfor", [128, 4])
    lam = din("lam", [4, 128])
    sg = din("sg", [128, 2])
    consts = din("consts", [128, 7 * 128])
    cosd = din("cos", [128, L])
    sind = din("sin", [128, L])
    out = nc.dram_tensor("out", [2048, D], F32, kind="ExternalOutput").ap()

    hT_own_q = [dint("hT_own%d" % q, [128, D], BF16) for q in range(16)]
    hT_all_q = [dint("hT_all%d" % q, [4 * 128, D], BF16) for q in range(16)]
    h_own = dint("h_own", [2048, D], F32)
    w_in_b = dint("w_in_b", [D, NCOL], BF16)
    if big:
        w_out_b = dint("w_out_b", [D, D], BF16)
        w_up_b = dint("w_up_b", [D, DFF], BF16)
        w_down_b = dint("w_down_b", [DFF, D], BF16)
    QKT = dint("QKT", [16 * 128, L], BF16)
    Vd = dint("Vd", [L, 1024], BF16)
    attn_own_q = [dint("attn_own%d" % q, [1024, 512], BF16) for q in range(16)]
    attn_all_q = [dint("attn_all%d" % q, [D, 512], BF16) for q in range(16)]

    S = Sched()

    def CMP(eng, fn, r=(), w=()):
        S.op(eng, fn, reads=r, writes=w)

    def DMA(q, o, i, r=(), w=()):
        S.op(q, lambda e, o=o, i=i: e.dma_start(out=o, in_=i), reads=r, writes=w, kind="d")

    keysets = {}

    def newkey(name):
        lst = keysets.setdefault(name, [])
        k = "%s#%d" % (name, len(lst))
        lst.append(k)
        return k

    def allkeys(name):
        return list(keysets.get(name, []))

    tok0 = nc.dram_tensor("tok0", [1, 1], mybir.dt.int32, kind="ExternalInput").ap()
    regs = {}

    def sp_prologue(e, es2):
        reg = es2.enter_context(e.register("tokreg"))
        e.reg_load(reg, tok0[0:1, 0:1])
        regs["off"] = e.snap(reg)
    S.sp_prologue = sp_prologue

    NA = 53000
    with (
        nc.sbuf_tensor("arena", [128, NA], F32) as A,
        nc.psum_tensor("ps", [128, 8 * 512], F32) as ps,
    ):
        class Alloc:
            def __init__(self):
                self.off = 0

            def f32(self, n):
                ap = A[:, self.off:self.off + n]
                self.off += n
                assert self.off <= NA, self.off
                return ap

            def bf16(self, n):
                m = (n + 1) // 2
                ap = A[:, self.off:self.off + m].bitcast(BF16)
                self.off += m
                assert self.off <= NA, self.off
                return ap[:, 0:n]

        al = Alloc()

        def bank(i):
            return ps[:, i * 512:(i + 1) * 512]

        def bankb(i):
            return ps[:, i * 512:(i + 1) * 512].bitcast(BF16)

        cst = al.f32(7 * 128)
        ident_f = cst[:, 0:128]
        tri_f = cst[:, 128:256]
        rot_f = cst[:, 256:384]
        e0_f = cst[:, 384:512]
        ones_f = cst[:, 640:768]
        mask16 = cst[:, 768:769]
        epst = cst[:, 769:770]
        onet = cst[:, 770:771]
        identb = al.bf16(128)
        trib = al.bf16(128)
        maskdb = al.bf16(128)
        onesb = al.bf16(128)
        st_l = [al.f32(48) for _ in range(3)]
        mv_l = [al.f32(2) for _ in range(3)]
        tmp1_l = [al.f32(1) for _ in range(3)]
        rstd_l = [al.f32(1) for _ in range(3)]
        ln_ctr = [0]
        bfort = al.f32(4)
        sgt = al.f32(2)
        nlam = al.f32(1)
        metaT = al.bf16(32 * 16).rearrange("p (k t) -> p k t", t=16)
        Zt = al.f32(4 * 65).rearrange("p (h t) -> p h t", t=65)
        Ct = al.f32(4 * 65).rearrange("p (h t) -> p h t", t=65)
        CREF = al.f32(4 * 65).rearrange("p (h t) -> p h t", t=65)
        scratch = al.f32(1)
        PERS = al.off

        def barrier(prev, new):
            CMP("dve", lambda e: e.memset(scratch, 0.0), w=list(prev) + list(new))

        dbg_b = dbg_f = None
        if stop_after is not None:
            dbg_b = nc.dram_tensor("dbg_b", [8, 128, 512], BF16, kind="ExternalOutput").ap()
            dbg_f = nc.dram_tensor("dbg_f", [4, 128, 512], F32, kind="ExternalOutput").ap()

        def stop(phase, dumps):
            if stop_after != phase:
                return
            ks = []
            for (kind, i, src, keys) in dumps:
                dst = (dbg_b if kind == "b" else dbg_f)[i]
                k = newkey("dbg")
                DMA("sp", dst, src, r=keys, w=[k])
            S.op("sp", None, reads=allkeys("dbg"), kind="w")
            S.emit(nc)
            raise _Stop()

        DMA("sp", cst, consts, w=["cst"])
        DMA("sp", bfort, bfor, w=["bfort"])
        DMA("sp", sgt, sg, w=["sgt"])
        CMP("dve", lambda e: e.tensor_copy(out=identb, in_=ident_f), r=["cst"], w=["identb"])
        CMP("dve", lambda e: e.tensor_copy(out=trib, in_=tri_f), r=["cst"], w=["trib"])
        CMP("dve", lambda e: e.tensor_copy(out=maskdb, in_=cst[:, 512:640]), r=["cst"], w=["maskdb"])
        CMP("dve", lambda e: e.tensor_copy(out=onesb, in_=ones_f), r=["cst"], w=["onesb"])

        win_groups = [(0, 1024), (1024, 2048), (2048, NCOL)]
        for i in range(8):
            DMA("pool", w_in_b[i * 512:(i + 1) * 512, 0:1024], w_in[i * 512:(i + 1) * 512, 0:1024], w=[newkey("w_in_b0")])

        def layer_norm(yt, np_, ykey, G, Bt, gbkey, yb=None, ybkey=None, hT_dst=None, hTkey=None, defer_T=False):
            li_ = ln_ctr[0] % 3
            ln_ctr[0] += 1
            st, mv, tmp1, rstd = st_l[li_], mv_l[li_], tmp1_l[li_], rstd_l[li_]
            kst, kmv, ktm, krs = "st%d" % li_, "mv%d" % li_, "tmp1%d" % li_, "rstd%d" % li_

            def f_stats(e):
                ins = None
                for c in range(8):
                    ins = e.bn_stats(out=st[:np_, c * 6:(c + 1) * 6], in_=yt[:, c * 512:(c + 1) * 512])
                return ins
            CMP("dve", f_stats, r=[ykey], w=[kst])
            CMP("dve", lambda e: e.bn_aggr(out=mv[:np_, :], in_=st[:np_, :]), r=[kst], w=[kmv])
            CMP("act", lambda e: e.activation(out=tmp1[:np_, :], in_=mv[:np_, 1:2], func=AF.Ln, bias=epst[:np_, :], scale=1.0),
                r=[kmv, "cst"], w=[ktm])
            CMP("act", lambda e: e.activation(out=rstd[:np_, :], in_=tmp1[:np_, :], func=AF.Exp, scale=-0.5),
                r=[ktm], w=[krs])
            CMP("dve", lambda e: e.scalar_tensor_tensor(out=yt, in0=yt, scalar=mv[:np_, 0:1], in1=G[:np_, :],
                                                        op0=ALU.subtract, op1=ALU.mult), r=[ykey, kmv, gbkey], w=[ykey])
            CMP("dve", lambda e: e.scalar_tensor_tensor(out=yt, in0=yt, scalar=rstd[:np_, :], in1=Bt[:np_, :],
                                                        op0=ALU.mult, op1=ALU.add), r=[ykey, krs, gbkey], w=[ykey])
            if yb is None:
                return
            CMP("dve", lambda e: e.tensor_copy(out=yb[:np_, :], in_=yt), r=[ykey], w=[ybkey])
            if defer_T:
                return
            ln_transposes(np_, yb, ybkey, hT_dst, hTkey)

        def ln_transposes(np_, yb, ybkey, hT_dst, hTkey):
            for g in range(4):
                bk = 6 + (g % 2)
                pb = bankb(bk)

                def f_tr(e, g=g, pb=pb):
                    ins = None
                    for kk in range(8):
                        k = g * 8 + kk
                        ins = e.transpose(pb[:, kk * 128:kk * 128 + np_], yb[:np_, k * 128:(k + 1) * 128], identb[:np_, :np_])
                    return ins
                CMP("pe", f_tr, r=[ybkey, "identb"], w=["ps%d" % bk])
                src = pb.rearrange("p (k t) -> p k t", t=128)[:, :, 0:np_]
                dst = hT_dst[:, g * 8:(g + 1) * 8, 0:np_]
                CMP("act", lambda e, s=src, d=dst: e.activation(out=d, in_=s, func=AF.Copy), r=["ps%d" % bk], w=[hTkey])

        G_in = al.f32(D)
        B_in = al.f32(D)
        yts = [al.f32(D) for _ in range(3)]
        ybs = [al.bf16(D) for _ in range(2)]
        hTs = [al.bf16(32 * 128).rearrange("p (k t) -> p k t", t=128) for _ in range(2)]
        DMA("sp", G_in, lnrow[0:1, :].partition_broadcast(128), w=["gb"])
        DMA("sp", B_in, lnrow[1:2, :].partition_broadcast(128), w=["gb"])
        def allgather(src, dst, groups, rkeys, wkey):
            S.op("pool", lambda e: e.collective_compute("AllGather", ALU.bypass, replica_groups=groups,
                                                        ins=[src.opt()], outs=[dst.opt()], dma_qos="P3"),
                 reads=rkeys, writes=[wkey], kind="cc")
        G4 = [[0, 1, 2, 3], [4, 5, 6, 7]]
        G8 = [list(range(8))]
        def load_x(i):
            DMA("sp", yts[i % 3], x[i * 128:(i + 1) * 128, :], w=["yt%d" % (i % 3)])
        load_x(0)
        load_x(1)
        layer_norm(yts[0], 128, "yt0", G_in, B_in, "gb", yb=ybs[0], ybkey="yb0", defer_T=True)
        for i in range(16):
            s = i % 3
            q = i % 2
            if i + 2 < 16:
                load_x(i + 2)
            if i + 1 < 16:
                layer_norm(yts[(i + 1) % 3], 128, "yt%d" % ((i + 1) % 3), G_in, B_in, "gb", yb=ybs[(i + 1) % 2], ybkey="yb%d" % ((i + 1) % 2),
                           defer_T=True)
            DMA("sp", h_own[i * 128:(i + 1) * 128, :], yts[s], r=["yt%d" % s], w=[newkey("h_own")])
            ln_transposes(128, ybs[q], "yb%d" % q, hTs[q], "hTs%d" % q)
            DMA("sp", hT_own_q[i], hTs[q].rearrange("p k t -> p (k t)"), r=["hTs%d" % q], w=["hT_own%d" % i])
            allgather(hT_own_q[i], hT_all_q[i], G4, ["hT_own%d" % i], "hT_all%d" % i)
        DMA("sp", yts[1][0:16, :], meta, w=["yt1"])
        layer_norm(yts[1][0:16, :], 16, "yt1", G_in, B_in, "gb", yb=ybs[0], ybkey="yb0", hT_dst=metaT, hTkey="metaT")
        stop("p1b", [("b", 0, hT_all_q[0][128:256, 0:512], ["hT_all0"]),
                     ("b", 1, hT_all_q[9][384:512, 1024:1536], ["hT_all9"])])

        bg = []
        for gi in (1, 2):
            c0_, c1_ = win_groups[gi]
            for i in range(8):
                bg.append((w_in_b[i * 512:(i + 1) * 512, c0_:c1_], w_in[i * 512:(i + 1) * 512, c0_:c1_], "w_in_b%d" % gi))
        if big:
            for (srcw, dstw, nm) in ((w_out_f, w_out_b, "w_out_b"), (w_up_f, w_up_b, "w_up_b"), (w_down_f, w_down_b, "w_down_b")):
                sv = srcw.rearrange("r (a c) -> (r a) c", c=2048)
                dv = dstw.rearrange("r (a c) -> (r a) c", c=2048)
                for i in range(sv.shape[0] // 512):
                    bg.append((dv[i * 512:(i + 1) * 512, :], sv[i * 512:(i + 1) * 512, :], nm))

        def drip(n, pace=()):
            for _ in range(n):
                if not bg:
                    return
                d_, s_src, nm = bg.pop(0)
                DMA("pool", d_, s_src, r=list(pace), w=[newkey(nm)])

        al.off = PERS
        lam4 = al.f32(4 * 128).rearrange("p (v d) -> p v d", d=128)
        prod = al.f32(2 * 128).rearrange("p (v d) -> p v d", d=128)
        dots = al.f32(2)
        eds = al.f32(2)
        barrier(["yt0", "yt1", "yt2", "gb", "yb0", "yb1", "hTs0", "hTs1"], ["lam4", "prod", "dots", "eds"])
        DMA("sp", lam4, lam.partition_broadcast(128), w=["lam4"])
        CMP("dve", lambda e: e.tensor_tensor(out=prod[:, 0, :], in0=lam4[:, 0, :], in1=lam4[:, 1, :], op=ALU.mult), r=["lam4"], w=["prod"])
        CMP("dve", lambda e: e.tensor_tensor(out=prod[:, 1, :], in0=lam4[:, 2, :], in1=lam4[:, 3, :], op=ALU.mult), r=["lam4"], w=["prod"])
        CMP("dve", lambda e: e.tensor_reduce(out=dots, in_=prod, axis=AX.X, op=ALU.add), r=["prod"], w=["dots"])
        CMP("act", lambda e: e.activation(out=eds, in_=dots, func=AF.Exp), r=["dots"], w=["eds"])
        CMP("dve", lambda e: e.tensor_scalar(out=nlam, in0=eds[:, 1:2], scalar1=eds[:, 0:1], scalar2=-LAMBDA_INIT,
                                             op0=ALU.subtract, op1=ALU.add), r=["eds"], w=["nlam"])

        al.off = PERS
        wres = al.bf16(32 * 1028)
        hTb = [al.bf16(32 * 512).rearrange("p (r k t) -> p r k t", k=32, t=128) for _ in range(2)]
        stg = [al.bf16(8 * 512).rearrange("p (j t) -> p j t", t=512) for _ in range(2)]
        vst = [al.bf16(1024) for _ in range(2)]
        Tf = [al.f32(512) for _ in range(2)]
        t1 = [al.f32(512) for _ in range(2)]
        t2 = [al.f32(512) for _ in range(2)]
        cosb = [al.f32(512) for _ in range(2)]
        sinb = [al.f32(512) for _ in range(2)]
        w_in_v = w_in_b.rearrange("(k p) c -> p k c", p=128)
        QKT_v = QKT.rearrange("(j p) t -> p j t", p=128)
        prev_keys = ["yt0", "yt1", "yt2", "gb", "yb0", "yb1", "hTs0", "hTs1", "lam4", "prod", "dots", "eds"]
        barrier(prev_keys, ["wres", "hTb0", "hTb1", "stg0", "stg1", "vst0", "vst1",
                            "Tf0", "Tf1", "t10", "t11", "t20", "t21", "cs0", "cs1"])

        CMP("dve", lambda e: e.memset(Zt.rearrange("p h t -> p (h t)"), 0.0), w=["Zt"])
        blocks = [(-1, 0, 16)] + [(tb, 16 + tb * 512, 512) for tb in range(16)]
        nblk = len(blocks)
        acc_bank = [0]
        rope_pending = [None]

        def load_hT(bi, slot):
            tb, col0, n = blocks[bi]
            if tb < 0:
                return
            for rr in range(4):
                DMA("sp", hTb[slot][:, rr].rearrange("p k t -> p (k t)"), hT_all_q[tb][rr * 128:(rr + 1) * 128, :],
                    r=["hT_all%d" % tb], w=["hTb%d" % slot])

        def hT_rhs(bi, slot, k, c0, n):
            tb = blocks[bi][0]
            if tb < 0:
                return metaT[:, k, c0:c0 + n]
            if n == 512:
                return hTb[slot][:, :, k, :]
            return hTb[slot][:, c0 // 128, k, :]

        def acc_out(bk, n):
            if n == 512:
                return bank(bk).rearrange("p (r t) -> p r t", t=128)
            return bank(bk)[:, 0:n]

        for pas in range(3):
            if pas < 2:
                wv = wres[:, 0:32 * 1024].rearrange("p (k c) -> p k c", c=1024)
                for g in range(4):
                    DMA("sp", wv[:, g * 8:(g + 1) * 8, :], w_in_v[:, g * 8:(g + 1) * 8, pas * 1024:(pas + 1) * 1024],
                        r=allkeys("w_in_b%d" % pas), w=["wres"])
            else:
                wv = wres.rearrange("p (k c) -> p k c", c=1028)
                for g in range(4):
                    DMA("sp", wv[:, g * 8:(g + 1) * 8, :], w_in_v[:, g * 8:(g + 1) * 8, 2048:3076], r=allkeys("w_in_b2"), w=["wres"])
            if rope_pending[0] is not None:
                rope_pending[0]()
                rope_pending[0] = None
            load_hT(1, 1)
            for bi in range(nblk):
                tb, col0, n = blocks[bi]
                slot = bi % 2
                if bi + 1 < nblk and bi >= 1:
                    load_hT(bi + 1, (bi + 1) % 2)
                hkey = "metaT" if tb < 0 else "hTb%d" % slot
                pk = (keysets.get("QKT", []) + keysets.get("Vd", []))[-1:]
                drip(2 if pas == 0 else 1, pk)
                if pas == 0:
                    DMA("sp", cosb[slot][:, 0:n], cosd[:, col0:col0 + n], w=["cs%d" % slot])
                    DMA("sp", sinb[slot][:, 0:n], sind[:, col0:col0 + n], w=["cs%d" % slot])
                if pas < 2:
                    sg_ = stg[slot]
                    for j in range(8):
                        bk = acc_bank[0] % 4
                        acc_bank[0] += 1

                        def f_mm(e, j=j, bk=bk, bi=bi, slot=slot, n=n, wv=wv):
                            ins = None
                            for k in range(32):
                                ins = e.matmul(acc_out(bk, n), lhsT=wv[:, k, j * 128:(j + 1) * 128], rhs=hT_rhs(bi, slot, k, 0, n),
                                               start=(k == 0), stop=(k == 31))
                            return ins
                        CMP("pe", f_mm, r=["wres", hkey], w=["ps%d" % bk])
                        if pas == 1:
                            if j % 2 == 0:
                                CMP("act", lambda e, j=j, bk=bk, n=n, sg_=sg_: e.activation(out=sg_[:, j, 0:n], in_=bank(bk)[:, 0:n], func=AF.Copy),
                                    r=["ps%d" % bk], w=["stg%d" % slot])
                            else:
                                CMP("dve", lambda e, j=j, bk=bk, n=n, sg_=sg_: e.tensor_copy(out=sg_[:, j, 0:n], in_=bank(bk)[:, 0:n]),
                                    r=["ps%d" % bk], w=["stg%d" % slot])
                        else:
                            u = j % 2
                            rb = 4 + u
                            CMP("act", lambda e, bk=bk, n=n, u=u: e.activation(out=Tf[u][:, 0:n], in_=bank(bk)[:, 0:n], func=AF.Copy),
                                r=["ps%d" % bk], w=["Tf%d" % u])
                            if rope_pending[0] is not None:
                                rope_pending[0]()

                            def rope_rest(n=n, u=u, rb=rb, slot=slot, j=j, sg_=sg_, pas=pas, col0=col0):
                                CMP("pe", lambda e: e.matmul(bank(rb)[:, 0:n], lhsT=rot_f, rhs=Tf[u][:, 0:n], start=True, stop=True),
                                    r=["Tf%d" % u, "cst"], w=["ps%d" % rb])
                                CMP("dve", lambda e: e.tensor_tensor(out=t1[u][:, 0:n], in0=Tf[u][:, 0:n], in1=cosb[slot][:, 0:n], op=ALU.mult),
                                    r=["Tf%d" % u, "cs%d" % slot], w=["t1%d" % u])
                                CMP("dve", lambda e: e.tensor_tensor(out=t2[u][:, 0:n], in0=bank(rb)[:, 0:n], in1=sinb[slot][:, 0:n], op=ALU.mult),
                                    r=["ps%d" % rb, "cs%d" % slot], w=["t2%d" % u])
                                CMP("dve", lambda e: e.tensor_tensor(out=sg_[:, j, 0:n], in0=t1[u][:, 0:n], in1=t2[u][:, 0:n], op=ALU.add),
                                    r=["t1%d" % u, "t2%d" % u], w=["stg%d" % slot])
                                if j == 7:
                                    DMA("sp", QKT_v[:, pas * 8:(pas + 1) * 8, col0:col0 + n], sg_[:, :, 0:n], r=["stg%d" % slot], w=[newkey("QKT")])
                            rope_pending[0] = rope_rest
                    if pas == 1:
                        DMA("sp", QKT_v[:, pas * 8:(pas + 1) * 8, col0:col0 + n], sg_[:, :, 0:n], r=["stg%d" % slot], w=[newkey("QKT")])
                else:
                    nsub = 1 if tb < 0 else 4
                    for s_ in range(nsub):
                        np_ = 16 if tb < 0 else 128
                        tix = 0 if tb < 0 else 1 + tb * 4 + s_
                        row0 = 0 if tb < 0 else 16 + (tb * 4 + s_) * 128
                        vs = (bi * 4 + s_) % 2

                        b0 = 3 * vs

                        def f_mm(e, bi=bi, slot=slot, s_=s_, np_=np_, wv=wv, b0=b0):
                            ins = None
                            for k in range(32):
                                lt = hT_rhs(bi, slot, k, s_ * 128 if np_ == 128 else 0, np_)
                                e.matmul(bank(b0)[:np_, :], lhsT=lt, rhs=wv[:, k, 0:512], start=(k == 0), stop=(k == 31))
                                e.matmul(bank(b0 + 1)[:np_, :], lhsT=lt, rhs=wv[:, k, 512:1024], start=(k == 0), stop=(k == 31))
                                ins = e.matmul(bank(b0 + 2)[:np_, 0:4], lhsT=lt, rhs=wv[:, k, 1024:1028], start=(k == 0), stop=(k == 31))
                            return ins
                        CMP("pe", f_mm, r=["wres", hkey], w=["ps%d" % b0, "ps%d" % (b0 + 1), "ps%d" % (b0 + 2)])
                        CMP("act", lambda e, np_=np_, vs=vs, b0=b0: e.activation(out=vst[vs][:np_, 0:512], in_=bank(b0)[:np_, :], func=AF.Copy),
                            r=["ps%d" % b0], w=["vst%d" % vs])
                        CMP("dve", lambda e, np_=np_, vs=vs, b0=b0: e.tensor_copy(out=vst[vs][:np_, 512:1024], in_=bank(b0 + 1)[:np_, :]),
                            r=["ps%d" % (b0 + 1)], w=["vst%d" % vs])
                        CMP("dve", lambda e, np_=np_, tix=tix, b0=b0: e.tensor_copy(out=Zt[:np_, :, tix], in_=bank(b0 + 2)[:np_, 0:4]),
                            r=["ps%d" % (b0 + 2)], w=["Zt"])
                        DMA("sp", Vd[row0:row0 + np_, :], vst[vs][:np_, :], r=["vst%d" % vs], w=[newkey("Vd")])

        stop("p2", [("b", 0, QKT[0:128, 0:512], allkeys("QKT")),
                    ("b", 1, QKT[9 * 128:10 * 128, 4096:4608], allkeys("QKT")),
                    ("b", 2, Vd[0:128, 0:512], allkeys("Vd")),
                    ("b", 3, Vd[4000:4128, 512:1024], allkeys("Vd"))])
        for h in range(4):
            CMP("dve", lambda e, h=h: e.tensor_scalar(out=Zt[:, h, :], in0=Zt[:, h, :], scalar1=bfort[:, h:h + 1], scalar2=None, op0=ALU.add),
                r=["Zt", "bfort"], w=["Zt"])
        Zf = Zt.rearrange("p h t -> p (h t)")
        Cf = Ct.rearrange("p h t -> p (h t)")
        CMP("act", lambda e: e.activation(out=Zf, in_=Zf, func=AF.Exp, scale=-1.0), r=["Zt"], w=["Zt"])
        CMP("act", lambda e: e.activation(out=Zf, in_=Zf, func=AF.Ln, bias=onet, scale=1.0), r=["Zt", "cst"], w=["Zt"])
        CMP("dve", lambda e: e.tensor_scalar(out=Zt[:, :, 0], in0=Zt[:, :, 0], scalar1=mask16, scalar2=None, op0=ALU.mult),
            r=["Zt", "cst"], w=["Zt"])
        CMP("pe", lambda e: e.matmul(bank(0)[:, 0:260], lhsT=tri_f, rhs=Zf, start=True, stop=True), r=["Zt", "cst"], w=["ps0"])
        CMP("pe", lambda e: e.matmul(bank(1)[:, 0:260], lhsT=ones_f, rhs=Zf, start=True, stop=True), r=["Zt", "cst"], w=["ps1"])
        TOT = CREF
        CMP("dve", lambda e: e.tensor_copy(out=TOT.rearrange("p h t -> p (h t)"), in_=bank(1)[:, 0:260]), r=["ps1"], w=["CREF"])
        PX = Zt
        onesrow = ones_f[:, 0:65]
        for h in range(4):
            CMP("dve", lambda e, h=h: e.tensor_tensor_scan(out=PX[:, h, :], data0=onesrow, data1=TOT[:, h, :], initial=0.0,
                                                           op0=ALU.mult, op1=ALU.add), r=["CREF", "Zt", "cst", "ps0", "ps1"], w=["Zt"])
        CMP("dve", lambda e: e.tensor_copy(out=Cf, in_=bank(0)[:, 0:260]), r=["ps0"], w=["Ct"])
        CMP("dve", lambda e: e.tensor_tensor(out=Ct[:, :, 1:65], in0=Ct[:, :, 1:65], in1=PX[:, :, 0:64], op=ALU.add), r=["Ct", "Zt"], w=["Ct"])
        CMP("pe", lambda e: e.matmul(bank(2)[:, 0:260], lhsT=e0_f, rhs=Cf, start=True, stop=True), r=["Ct", "cst"], w=["ps2"])
        CMP("dve", lambda e: e.tensor_copy(out=CREF.rearrange("p h t -> p (h t)"), in_=bank(2)[:, 0:260]), r=["ps2", "Zt"], w=["CREF"])

        al.off = PERS
        QK = al.bf16(4 * L).rearrange("p (m t) -> p m t", t=L)
        Vs = al.bf16(65 * 256)
        PT = [al.bf16(512) for _ in range(6)]
        ob = [al.bf16(2 * 512).rearrange("p (c t) -> p c t", t=512) for _ in range(2)]
        BT = al.f32(64 * 65).rearrange("p (q k) -> p q k", k=65)
        rec = [al.f32(512) for _ in range(2)]
        o1n = al.f32(1024).rearrange("p (c t) -> p c t", t=512)
        o2n = al.f32(1024).rearrange("p (c t) -> p c t", t=512)
        dd = al.f32(1024).rearrange("p (c t) -> p c t", t=512)
        dsq = al.f32(1024).rearrange("p (c t) -> p c t", t=512)
        rsb = al.f32(512)
        abias = al.f32(65)
        p2keys = ["wres", "hTb0", "hTb1", "stg0", "stg1", "vst0", "vst1", "Tf0", "Tf1", "t10", "t11", "t20", "t21", "cs0", "cs1"]
        barrier(p2keys, ["QK", "Vs", "BT"] + ["pt%d" % i for i in range(6)] + ["ob0", "ob1", "rec0", "rec1", "o1n", "o2n", "dd", "dsq", "rsb", "abias"])
        pt_ctr = [0]
        ob_ctr = [0]
        last_attn = [None]

        def load_head(chunks, vc0, dv):
            for m, ch in enumerate(chunks):
                DMA("sp", QK[:, m, :], QKT[ch * 128:(ch + 1) * 128, :], r=allkeys("QKT"), w=["QK"])
            Vv = Vs[:, 0:65 * dv].rearrange("p (t c) -> p t c", c=dv)
            DMA("sp", Vv[0:16, 0, :], Vd[0:16, vc0:vc0 + dv], r=allkeys("Vd"), w=["Vs"])
            Vsrc = Vd[16:L, vc0:vc0 + dv].rearrange("(t p) c -> p t c", p=128)
            for g in range(4):
                DMA("sp", Vv[:, 1 + g * 16:1 + (g + 1) * 16, :], Vsrc[:, g * 16:(g + 1) * 16, :], r=allkeys("Vd"), w=["Vs"])
            return Vv

        for hh in range(2):
            Vv = load_head([4 * hh, 4 * hh + 1, 4 * hh + 2, 4 * hh + 3], 256 * hh, 256)
            for J in range(16):
                qc0 = 16 + 512 * J
                ktiles = [-1] + list(range(4 * J + 4))
                pend = []
                for kt in ktiles:
                    kp = 16 if kt < 0 else 128
                    kc0 = 0 if kt < 0 else 16 + 128 * kt
                    i = -1 if kt < 0 else kt - 4 * J
                    c0 = max(i, 0) * 128
                    first = kt < 0
                    last = kt == 4 * J + 3
                    cur = []
                    for m in range(2):
                        sb = m
                        slot = pt_ctr[0] % 6
                        pt_ctr[0] += 1
                        CMP("pe", lambda e, sb=sb, kp=kp, c0=c0, kc0=kc0, m=m, qc0=qc0: e.matmul(
                            bank(sb)[:kp, c0:512], lhsT=QK[:, 2 + m, kc0:kc0 + kp], rhs=QK[:, m, qc0 + c0:qc0 + 512], start=True, stop=True),
                            r=["QK"], w=["ps%d" % sb])
                        CMP("act", lambda e, sb=sb, kp=kp, c0=c0, slot=slot: e.activation(
                            out=PT[slot][:kp, c0:512], in_=bank(sb)[:kp, c0:512], func=AF.Exp, scale=SCALE),
                            r=["ps%d" % sb], w=["pt%d" % slot])
                        if i >= 0:
                            CMP("dve", lambda e, slot=slot, c0=c0: e.tensor_tensor(
                                out=PT[slot][:, c0:c0 + 128], in0=PT[slot][:, c0:c0 + 128], in1=maskdb, op=ALU.mult),
                                r=["pt%d" % slot, "maskdb"], w=["pt%d" % slot])

                        def f_av(e, m=m, kp=kp, c0=c0, slot=slot, kt=kt, first=first, last=last, Vv=Vv):
                            for c in range(2):
                                e.matmul(bank(2 + 2 * m + c)[:, c0:512], lhsT=Vv[:kp, kt + 1, c * 128:(c + 1) * 128],
                                         rhs=PT[slot][:kp, c0:512], start=first, stop=last)
                            return e.matmul(bank(6 + m)[:, c0:512], lhsT=onesb[:kp, :], rhs=PT[slot][:kp, c0:512], start=first, stop=last)
                        cur.append((f_av, ["pt%d" % slot, "Vs", "onesb"], ["ps%d" % (2 + 2 * m), "ps%d" % (3 + 2 * m), "ps%d" % (6 + m)]))
                    for (fn_, r_, w_) in pend:
                        CMP("pe", fn_, r=r_, w=w_)
                    pend = cur
                for (fn_, r_, w_) in pend:
                    CMP("pe", fn_, r=r_, w=w_)
                osl = ob_ctr[0] % 2
                drip(1, [last_attn[0]] if last_attn[0] else [])
                ob_ctr[0] += 1
                for m_ in range(2):
                    CMP("act", lambda e, m_=m_: e.activation(out=rec[m_], in_=bank(6 + m_), func=AF.Ln), r=["ps%d" % (6 + m_)], w=["rec%d" % m_])
                    CMP("act", lambda e, m_=m_: e.activation(out=rec[m_], in_=rec[m_], func=AF.Exp, scale=-1.0), r=["rec%d" % m_], w=["rec%d" % m_])
                for c in range(2):
                    CMP("dve", lambda e, c=c: e.tensor_tensor(out=o1n[:, c, :], in0=bank(2 + c), in1=rec[0], op=ALU.mult),
                        r=["ps%d" % (2 + c), "rec0"], w=["o1n"])
                    CMP("dve", lambda e, c=c: e.tensor_tensor(out=o2n[:, c, :], in0=bank(4 + c), in1=rec[1], op=ALU.mult),
                        r=["ps%d" % (4 + c), "rec1"], w=["o2n"])
                ddf = dd.rearrange("p c t -> p (c t)")
                CMP("dve", lambda e, ddf=ddf: e.scalar_tensor_tensor(out=ddf, in0=o2n.rearrange("p c t -> p (c t)"), scalar=nlam,
                                                                     in1=o1n.rearrange("p c t -> p (c t)"), op0=ALU.mult, op1=ALU.add),
                    r=["o1n", "o2n", "nlam"], w=["dd"])
                CMP("act", lambda e, ddf=ddf: e.activation(out=dsq.rearrange("p c t -> p (c t)"), in_=ddf, func=AF.Square), r=["dd"], w=["dsq"])

                def f_ss(e):
                    e.matmul(bank(6), lhsT=ones_f, rhs=dsq[:, 0, :], start=True, stop=False)
                    return e.matmul(bank(6), lhsT=ones_f, rhs=dsq[:, 1, :], start=False, stop=True)
                CMP("pe", f_ss, r=["dsq", "cst"], w=["ps6"])
                CMP("act", lambda e: e.activation(out=rsb, in_=bank(6), func=AF.Ln, bias=epst, scale=1.0 / 256.0), r=["ps6", "cst"], w=["rsb"])
                CMP("act", lambda e: e.activation(out=rsb, in_=rsb, func=AF.Exp, scale=-0.5), r=["rsb"], w=["rsb"])
                for c in range(2):
                    CMP("dve", lambda e, c=c: e.tensor_tensor(out=dd[:, c, :], in0=dd[:, c, :], in1=rsb, op=ALU.mult), r=["dd", "rsb"], w=["dd"])
                    CMP("dve", lambda e, c=c, osl=osl: e.tensor_scalar(out=ob[osl][:, c, :], in0=dd[:, c, :], scalar1=sgt[:, c:c + 1],
                                                                       scalar2=1.0 - LAMBDA_INIT, op0=ALU.mult, op1=ALU.mult),
                        r=["dd", "sgt"], w=["ob%d" % osl])
                last_attn[0] = newkey("attn_own%d" % J)
                DMA("sp", attn_own_q[J][256 * hh:256 * hh + 256, :].rearrange("(c p) t -> p c t", p=128), ob[osl],
                    r=["ob%d" % osl], w=[last_attn[0]])

        for fh in range(4):
            Vv = load_head([8 + 2 * fh, 9 + 2 * fh], 512 + 128 * fh, 128)
            CMP("dve", lambda e, fh=fh: e.tensor_tensor(out=abias, in0=Ct[:, fh, :], in1=CREF[:, fh, :], op=ALU.subtract),
                r=["Ct", "CREF"], w=["abias"])
            for qt in range(64):
                CMP("dve", lambda e, qt=qt, fh=fh: e.tensor_scalar(out=BT[:, qt, :], in0=CREF[:, fh, :], scalar1=CREF[:, fh, qt + 1:qt + 2],
                                                                   scalar2=None, op0=ALU.subtract), r=["CREF"], w=["BT"])
            BTf = BT.rearrange("p q k -> p (q k)")
            CMP("act", lambda e, BTf=BTf: e.activation(out=BTf, in_=BTf, func=AF.Exp), r=["BT"], w=["BT"])
            for J in range(16):
                qc0 = 16 + 512 * J
                par = J % 2
                bo = 2 + par
                bl = 4 + par
                ktiles = [-1] + list(range(4 * J + 4))
                pend = []
                for kt in ktiles:
                    kp = 16 if kt < 0 else 128
                    kc0 = 0 if kt < 0 else 16 + 128 * kt
                    i = -1 if kt < 0 else kt - 4 * J
                    c0 = max(i, 0) * 128
                    first = kt < 0
                    last = kt == 4 * J + 3
                    sb = pt_ctr[0] % 2
                    slot = pt_ctr[0] % 6
                    pt_ctr[0] += 1
                    CMP("pe", lambda e, sb=sb, kp=kp, c0=c0, kc0=kc0, qc0=qc0: e.matmul(
                        bank(sb)[:kp, c0:512], lhsT=QK[:, 1, kc0:kc0 + kp], rhs=QK[:, 0, qc0 + c0:qc0 + 512], start=True, stop=True),
                        r=["QK"], w=["ps%d" % sb])

                    CMP("act", lambda e, sb=sb, kp=kp, c0=c0, slot=slot, kt=kt: e.activation(
                        out=PT[slot][:kp, c0:512], in_=bank(sb)[:kp, c0:512], func=AF.Exp, bias=abias[:kp, kt + 1:kt + 2], scale=SCALE),
                        r=["ps%d" % sb, "abias"], w=["pt%d" % slot])

                    def f_scale(e, kp=kp, c0=c0, slot=slot, kt=kt, J=J):
                        q0 = c0 // 128
                        nq = 4 - q0
                        pv = PT[slot][:kp, c0:512].rearrange("p (q t) -> p q t", t=128)
                        wv_ = BT[:kp, 4 * J + q0:4 * J + 4, kt + 1].unsqueeze(2).to_broadcast([kp, nq, 128])
                        return e.tensor_tensor(out=pv, in0=pv, in1=wv_, op=ALU.mult)
                    CMP("dve", f_scale, r=["pt%d" % slot, "BT"], w=["pt%d" % slot])
                    if i >= 0:
                        CMP("dve", lambda e, slot=slot, c0=c0: e.tensor_tensor(
                            out=PT[slot][:, c0:c0 + 128], in0=PT[slot][:, c0:c0 + 128], in1=trib, op=ALU.mult),
                            r=["pt%d" % slot, "trib"], w=["pt%d" % slot])

                    def f_av(e, kp=kp, c0=c0, slot=slot, kt=kt, first=first, last=last, Vv=Vv, bo=bo, bl=bl):
                        e.matmul(bank(bo)[:, c0:512], lhsT=Vv[:kp, kt + 1, :], rhs=PT[slot][:kp, c0:512], start=first, stop=last)
                        return e.matmul(bank(bl)[:, c0:512], lhsT=onesb[:kp, :], rhs=PT[slot][:kp, c0:512], start=first, stop=last)
                    if len(pend) == 2:
                        fn_, r_, w_ = pend.pop(0)
                        CMP("pe", fn_, r=r_, w=w_)
                    pend.append((f_av, ["pt%d" % slot, "Vs", "onesb"], ["ps%d" % bo, "ps%d" % bl]))
                for (fn_, r_, w_) in pend:
                    CMP("pe", fn_, r=r_, w=w_)
                osl = ob_ctr[0] % 2
                drip(1, [last_attn[0]] if last_attn[0] else [])
                ob_ctr[0] += 1
                CMP("dve", lambda e, par=par, bl=bl: e.reciprocal(out=rec[par], in_=bank(bl)), r=["ps%d" % bl], w=["rec%d" % par])
                CMP("dve", lambda e, par=par, bo=bo, osl=osl: e.tensor_tensor(out=ob[osl][:, 0, :], in0=bank(bo), in1=rec[par], op=ALU.mult),
                    r=["ps%d" % bo, "rec%d" % par], w=["ob%d" % osl])
                last_attn[0] = newkey("attn_own%d" % J)
                DMA("sp", attn_own_q[J][512 + 128 * fh:512 + 128 * fh + 128, :], ob[osl][:, 0, :],
                    r=["ob%d" % osl], w=[last_attn[0]])
                if fh == 3:
                    allgather(attn_own_q[J], attn_all_q[J], G4, allkeys("attn_own%d" % J), "attn_all%d" % J)

        drip(10000)
        all_attn = [k for J in range(16) for k in allkeys("attn_own%d" % J)]
        stop("p3", [("b", 0, attn_own_q[0][0:128, :], all_attn),
                    ("b", 1, attn_own_q[9][384:512, :], all_attn),
                    ("b", 2, attn_own_q[0][512:640, :], all_attn),
                    ("b", 3, attn_own_q[15][896:1024, :], all_attn)])

        al.off = PERS
        actT = al.bf16(128 * 256).rearrange("p (f t) -> p f t", t=256)
        atT = actT.rearrange("p f t -> p (f t)")[:, 0:32 * 256].rearrange("p (k t) -> p k t", t=256)
        ybM = actT.rearrange("p f t -> p (f t)")[:, 32 * 256:32 * 256 + D]
        h1T = al.bf16(32 * 256).rearrange("p (k t) -> p k t", t=256)
        ring = [al.bf16(32 * 256) for _ in range(3)]
        ytm = [al.f32(D) for _ in range(2)]
        Gm = al.f32(D)
        Bm = al.f32(D)
        rbuf = [al.f32(512) for _ in range(2)]
        p3keys = ["QK", "Vs", "BT", "abias"] + ["pt%d" % i for i in range(6)] + ["ob0", "ob1", "rec0", "rec1", "o1n", "o2n", "dd", "dsq", "rsb"]
        barrier(p3keys, ["actT", "h1T", "ring0", "ring1", "ring2", "ytm0", "ytm1", "gbm", "rbuf0", "rbuf1"])
        w_out_v = w_out_b.rearrange("(k p) c -> p k c", p=128)
        w_up_v = w_up_b.rearrange("(k p) c -> p k c", p=128)
        w_down_v = w_down_b.rearrange("(f p) c -> p f c", p=128)
        attn_v = [t.rearrange("(k p) t -> p k t", p=128) for t in attn_all_q]
        loads = []
        for blk in range(8):
            for cg in range(4):
                for kq in range(4):
                    loads.append((w_out_v[:, kq * 8:(kq + 1) * 8, cg * 1024:(cg + 1) * 1024], "w_out_b"))
            for fg in range(16):
                for kq in range(4):
                    loads.append((w_up_v[:, kq * 8:(kq + 1) * 8, fg * 1024:(fg + 1) * 1024], "w_up_b"))
            for cg in range(4):
                for pc in range(16):
                    loads.append((w_down_v[:, pc * 8:(pc + 1) * 8, cg * 1024:(cg + 1) * 1024], "w_down_b"))
        issued = [0]

        def slot_view(i):
            return ring[i % 3].rearrange("p (k c) -> p k c", c=1024)

        def need(i):
            while issued[0] <= min(i + 2, len(loads) - 1):
                j = issued[0]
                DMA("sp", slot_view(j), loads[j][0], r=allkeys(loads[j][1]), w=["ring%d" % (j % 3)])
                issued[0] += 1

        li = 0
        for blk in range(8):
            need(li)
            for s_ in range(2):
                def f_at(e, blk=blk, s_=s_):
                    return e.dma_start(out=atT[:, :, s_ * 128:(s_ + 1) * 128], in_=attn_v[2 * blk + s_][:, :, bass.ds(regs["off"], 128)])
                S.op("sp", f_at, reads=["attn_all%d" % (2 * blk + s_)], writes=["actT"], kind="d")
            for s_ in range(2):
                DMA("sp", ytm[s_], h_own[blk * 256 + s_ * 128:blk * 256 + (s_ + 1) * 128, :], r=allkeys("h_own"), w=["ytm%d" % s_])
            DMA("sp", Gm, lnrow[2:3, :].partition_broadcast(128), w=["gbm"])
            DMA("sp", Bm, lnrow[3:4, :].partition_broadcast(128), w=["gbm"])
            for cg in range(4):
                for kq in range(4):
                    need(li)
                    wv = slot_view(li)
                    rk = "ring%d" % (li % 3)
                    li += 1
                    for s_ in range(2):
                        for hf in range(2):
                            bk = 4 * (cg % 2) + 2 * s_ + hf

                            def f_mm(e, wv=wv, s_=s_, hf=hf, bk=bk, kq=kq):
                                ins = None
                                for kk in range(8):
                                    ins = e.matmul(bank(bk), lhsT=atT[:, kq * 8 + kk, s_ * 128:(s_ + 1) * 128],
                                                   rhs=wv[:, kk, hf * 512:(hf + 1) * 512],
                                                   start=(kq == 0 and kk == 0), stop=(kq == 3 and kk == 7))
                                return ins
                            CMP("pe", f_mm, r=["actT", rk], w=["ps%d" % bk])
                for s_ in range(2):
                    for hf in range(2):
                        bk = 4 * (cg % 2) + 2 * s_ + hf
                        c0 = cg * 1024 + hf * 512
                        CMP("dve", lambda e, s_=s_, bk=bk, c0=c0: e.scalar_tensor_tensor(
                            out=ytm[s_][:, c0:c0 + 512], in0=ytm[s_][:, c0:c0 + 512], scalar=ALPHA,
                            in1=bank(bk), op0=ALU.mult, op1=ALU.add), r=["ps%d" % bk, "ytm%d" % s_], w=["ytm%d" % s_])
            for s_ in range(2):
                layer_norm(ytm[s_], 128, "ytm%d" % s_, Gm, Bm, "gbm", yb=ybM, ybkey="actT",
                           hT_dst=h1T[:, :, s_ * 128:(s_ + 1) * 128], hTkey="h1T")
            DMA("sp", Gm, lnrow[4:5, :].partition_broadcast(128), w=["gbm"])
            DMA("sp", Bm, lnrow[5:6, :].partition_broadcast(128), w=["gbm"])
            for fg in range(16):
                b0 = 4 * (fg % 2)
                for kq in range(4):
                    need(li)
                    wv = slot_view(li)
                    rk = "ring%d" % (li % 3)
                    li += 1
                    for f8 in range(8):
                        bk = b0 + f8 // 2
                        cc0 = (f8 % 2) * 256

                        def f_mm(e, wv=wv, f8=f8, bk=bk, cc0=cc0, kq=kq):
                            ins = None
                            for kk in range(8):
                                ins = e.matmul(bank(bk)[:, cc0:cc0 + 256], lhsT=wv[:, kk, f8 * 128:(f8 + 1) * 128],
                                               rhs=h1T[:, kq * 8 + kk, :], start=(kq == 0 and kk == 0 and f8 % 2 == 0),
                                               stop=(kq == 3 and kk == 7), skip_group_check=True)
                            return ins
                        CMP("pe", f_mm, r=["h1T", rk], w=["ps%d" % bk])
                for pr in range(4):
                    bk = b0 + pr
                    rb = pr % 2
                    f = fg * 8 + 2 * pr
                    CMP("act", lambda e, bk=bk, rb=rb: e.activation(out=rbuf[rb], in_=bank(bk), func=AF.Relu),
                        r=["ps%d" % bk], w=["rbuf%d" % rb])
                    CMP("pool", lambda e, rb=rb, f=f: e.tensor_tensor(out=actT[:, f:f + 2, :].rearrange("p f t -> p (f t)"), in0=rbuf[rb], in1=rbuf[rb], op=ALU.mult),
                        r=["rbuf%d" % rb], w=["actT"])
            for cg in range(4):
                for pc in range(16):
                    need(li)
                    wv = slot_view(li)
                    rk = "ring%d" % (li % 3)
                    li += 1
                    for s_ in range(2):
                        for hf in range(2):
                            bk = 4 * (cg % 2) + 2 * s_ + hf

                            def f_mm(e, wv=wv, s_=s_, hf=hf, bk=bk, pc=pc):
                                ins = None
                                for kk in range(8):
                                    ins = e.matmul(bank(bk), lhsT=actT[:, pc * 8 + kk, s_ * 128:(s_ + 1) * 128],
                                                   rhs=wv[:, kk, hf * 512:(hf + 1) * 512],
                                                   start=(pc == 0 and kk == 0), stop=(pc == 15 and kk == 7))
                                return ins
                            CMP("pe", f_mm, r=["actT", rk], w=["ps%d" % bk])
                for s_ in range(2):
                    for hf in range(2):
                        bk = 4 * (cg % 2) + 2 * s_ + hf
                        c0 = cg * 1024 + hf * 512
                        CMP("dve", lambda e, s_=s_, bk=bk, c0=c0: e.scalar_tensor_tensor(
                            out=ytm[s_][:, c0:c0 + 512], in0=ytm[s_][:, c0:c0 + 512], scalar=ALPHA,
                            in1=bank(bk), op0=ALU.mult, op1=ALU.add), r=["ps%d" % bk, "ytm%d" % s_], w=["ytm%d" % s_])
            for s_ in range(2):
                layer_norm(ytm[s_], 128, "ytm%d" % s_, Gm, Bm, "gbm")
                DMA("sp", out[blk * 256 + s_ * 128:blk * 256 + (s_ + 1) * 128, :], ytm[s_], r=["ytm%d" % s_], w=[newkey("out")])
        S.op("sp", None, reads=allkeys("out"), kind="w")
        S.emit(nc)


_NC_CACHE = {}


def _consts():
    c = np.zeros((128, 7, 128), np.float32)
    c[:, 0, :] = np.eye(128, dtype=np.float32)
    c[:, 1, :] = np.triu(np.ones((128, 128), np.float32))
    rot = np.zeros((128, 128), np.float32)
    for m in range(64):
        rot[m + 64, m] = -1.0
        rot[m, m + 64] = 1.0
    c[:, 2, :] = rot
    c[0, 3, :] = 1.0
    md = np.ones((128, 128), np.float32)
    md[64:, :64] = 0.0
    c[:, 4, :] = md
    c[:, 5, :] = 1.0
    c[:16, 6, 0] = 1.0
    c[:, 6, 1] = EPS
    c[:, 6, 2] = 1.0
    return np.ascontiguousarray(c.reshape(128, 7 * 128))


def _rope():
    inv = (1.0 / (np.float32(10000.0) ** (np.arange(0, HD, 2, dtype=np.float32) / np.float32(HD)))).astype(np.float32)
    ang = (np.arange(L, dtype=np.float32)[:, None] * inv[None, :]).astype(np.float32)
    ang = np.concatenate([ang, ang], axis=-1)
    return np.ascontiguousarray(np.cos(ang).T.astype(np.float32)), np.ascontiguousarray(np.sin(ang).T.astype(np.float32))


def _prep(x, meta_tokens, ln_in_g, ln_in_b, w_in, b_forget, lambda_q1, lambda_k1, lambda_q2, lambda_k2,
          subln_g, w_out, ln_attn_g, ln_attn_b, w_up, w_down, ln_mlp_g, ln_mlp_b):
    f = lambda a: np.asarray(a, dtype=np.float32)
    x = f(x)
    w_in0 = f(w_in)[0]
    w_out0 = f(w_out)[0]
    w_up0 = f(w_up)[0]
    w_down0 = f(w_down)[0]
    lnrow = np.ascontiguousarray(np.stack([f(ln_in_g), f(ln_in_b), f(ln_attn_g)[0], f(ln_attn_b)[0], f(ln_mlp_g)[0], f(ln_mlp_b)[0]]))
    lamv = np.ascontiguousarray(np.stack([f(lambda_q1)[0], f(lambda_k1)[0], f(lambda_q2)[0], f(lambda_k2)[0]]))
    sgv = np.ascontiguousarray(f(subln_g)[0].reshape(2, 128).T)
    consts = _consts()
    cosv, sinv = _rope()
    rows = []
    for rr in range(4):
        for h in (2 * rr, 2 * rr + 1):
            rows.extend(range(h * 256, (h + 1) * 256))
        for fh in range(4 * rr, 4 * rr + 4):
            rows.extend(range(2048 + fh * 128, 2048 + (fh + 1) * 128))
    w_out_p = np.ascontiguousarray(w_out0[np.array(rows)])
    w_up0 = np.ascontiguousarray(w_up0)
    w_down0 = np.ascontiguousarray(w_down0)
    in_maps = []
    for c in range(8):
        b, r = c // 4, c % 4
        cols = []
        for h in (2 * r, 2 * r + 1):
            for off in (DQ_OFF, DK_OFF):
                for m in range(2):
                    cols.extend(range(off + (2 * h + m) * 128, off + (2 * h + m + 1) * 128))
        for fh in range(4 * r, 4 * r + 4):
            cols.extend(range(FQ_OFF + fh * 128, FQ_OFF + (fh + 1) * 128))
            cols.extend(range(FK_OFF + fh * 128, FK_OFF + (fh + 1) * 128))
        for h in (2 * r, 2 * r + 1):
            cols.extend(range(DV_OFF + h * 256, DV_OFF + (h + 1) * 256))
        for fh in range(4 * r, 4 * r + 4):
            cols.extend(range(FV_OFF + fh * 128, FV_OFF + (fh + 1) * 128))
        for fh in range(4 * r, 4 * r + 4):
            cols.append(FF_OFF + fh)
        assert len(cols) == NCOL
        in_maps.append({
            "x": np.ascontiguousarray(x[b].reshape(16, 4, 128, D)[:, r].reshape(2048, D)),
            "meta": f(meta_tokens),
            "lnrow": lnrow,
            "w_in": np.ascontiguousarray(w_in0[:, np.array(cols)]),
            "w_out_f": w_out_p,
            "w_up_f": w_up0,
            "w_down_f": w_down0,
            "bfor": np.ascontiguousarray(np.broadcast_to(f(b_forget)[0][4 * r:4 * r + 4][None, :], (128, 4))),
            "lam": lamv,
            "sg": sgv,
            "consts": consts,
            "cos": cosv,
            "sin": sinv,
            "tok0": np.array([[128 * r]], np.int32),
        })
    return in_maps


def kernel(**inputs):
    in_maps = _prep(**inputs)
    if "nc" not in _NC_CACHE:
        _NC_CACHE["nc"] = build_nc()
    nc = _NC_CACHE["nc"]
    res = run_bass_kernel_spmd(nc, in_maps, core_ids=list(range(8)))
    outp = np.empty((2, SEQ, D), np.float32)
    for c in range(8):
        b, r = c // 4, c % 4
        outp[b].reshape(16, 4, 128, D)[:, r] = np.asarray(res.results[c]["out"], dtype=np.float32).reshape(16, 128, D)
    return outp
```
